# Optimizing a Trainium2 kernel written in Bass

```python
import math
import jax, jax.numpy as jnp
from jax import lax
import numpy as np

D_MODEL = 2048
BATCH = 2
SEQ = 8192
DEPTH = 4

N_FOURIER_GROUPS = 4
FOURIER_GROUP_DIM = D_MODEL // 8
D_FOURIER = N_FOURIER_GROUPS * FOURIER_GROUP_DIM
N_HEADS = 8
QK_HEAD_DIM = D_MODEL // 32
V_HEAD_DIM = 2 * QK_HEAD_DIM
D_QK = N_HEADS * 2 * QK_HEAD_DIM
D_V = N_HEADS * V_HEAD_DIM
ROPE_DIM = QK_HEAD_DIM // 4
ROPE_THETA = 500000.0
D_FF = ((8 * D_MODEL // 3 + 255) // 256) * 256
D_IN = D_FOURIER + 2 * D_QK + D_V + 2 * D_MODEL
Q_BLOCK = 128
EPS = 1e-6
LAMBDA_STD = 0.1

kernel_name = 'hybrid_fourier_diffattn_macaron_encoder'


def rms_norm(x, g):
    xf = x.astype(jnp.float32)
    y = xf * lax.rsqrt(jnp.mean(xf * xf, axis=-1, keepdims=True) + EPS)
    return (y * g.astype(jnp.float32)).astype(x.dtype)


def swiglu(h, w_gate, w_up, w_down):
    return (jax.nn.silu(h @ w_gate) * (h @ w_up)) @ w_down


def rope_tables(s):
    pos = jnp.arange(s, dtype=jnp.float32)
    inv_freq = ROPE_THETA ** (-jnp.arange(0, ROPE_DIM, 2, dtype=jnp.float32) / ROPE_DIM)
    ang = pos[:, None] * inv_freq[None, :]
    return jnp.cos(ang)[None, :, None, None, :], jnp.sin(ang)[None, :, None, None, :]


def partial_rope(x, cos, sin):
    half = ROPE_DIM // 2
    xr = x[..., :ROPE_DIM].astype(jnp.float32)
    x1, x2 = xr[..., :half], xr[..., half:]
    rot = jnp.concatenate([x1 * cos - x2 * sin, x2 * cos + x1 * sin], axis=-1)
    return jnp.concatenate([rot.astype(x.dtype), x[..., ROPE_DIM:]], axis=-1)


def fourier_mix(u):
    b, s, _ = u.shape
    ug = u.reshape(b, s, N_FOURIER_GROUPS, FOURIER_GROUP_DIM).astype(jnp.float32)
    f = jnp.fft.fft2(ug, axes=(1, 3), norm='ortho').real
    return f.reshape(b, s, D_FOURIER).astype(u.dtype)


def diff_attention(q, k, v, lam):
    b, s = q.shape[0], q.shape[1]
    nb = s // Q_BLOCK
    scale = QK_HEAD_DIM ** -0.5
    qb = jnp.moveaxis(q.reshape(b, nb, Q_BLOCK, N_HEADS, 2, QK_HEAD_DIM), 1, 0)

    def block(qi):
        sc = jnp.einsum('bqhcd,bkhcd->bhcqk', qi, k, preferred_element_type=jnp.float32) * scale
        p = jax.nn.softmax(sc, axis=-1)
        w = p[:, :, 0] - lam * p[:, :, 1]
        return jnp.einsum('bhqk,bkhe->bqhe', w.astype(v.dtype), v)

    o = lax.map(block, qb)
    return jnp.moveaxis(o, 0, 1).reshape(b, s, N_HEADS, V_HEAD_DIM)


def setup_inputs(seed: int = 0) -> dict:
    key = jax.random.key(seed)
    ks = jax.random.split(key, 24)

    def w(k, shape, fan_in):
        return jax.random.normal(k, shape, jnp.float32) * (fan_in ** -0.5)

    def gain(k, n):
        return 1.0 + 0.02 * jax.random.normal(k, (DEPTH, n), jnp.float32)

    return {
        'x': jax.random.normal(ks[0], (BATCH, SEQ, D_MODEL), jnp.float32),
        'norm_ffa': gain(ks[1], D_MODEL),
        'ffa_gate': w(ks[2], (DEPTH, D_MODEL, D_FF), D_MODEL),
        'ffa_up': w(ks[3], (DEPTH, D_MODEL, D_FF), D_MODEL),
        'ffa_down': w(ks[4], (DEPTH, D_FF, D_MODEL), D_FF),
        'norm_mix': gain(ks[5], D_MODEL),
        'w_in': w(ks[6], (DEPTH, D_MODEL, D_IN), D_MODEL),
        'q_norm': gain(ks[7], QK_HEAD_DIM),
        'k_norm': gain(ks[8], QK_HEAD_DIM),
        'lambda_q1': LAMBDA_STD * jax.random.normal(ks[9], (DEPTH, QK_HEAD_DIM), jnp.float32),
        'lambda_k1': LAMBDA_STD * jax.random.normal(ks[10], (DEPTH, QK_HEAD_DIM), jnp.float32),
        'lambda_q2': LAMBDA_STD * jax.random.normal(ks[11], (DEPTH, QK_HEAD_DIM), jnp.float32),
        'lambda_k2': LAMBDA_STD * jax.random.normal(ks[12], (DEPTH, QK_HEAD_DIM), jnp.float32),
        'subln': gain(ks[13], V_HEAD_DIM),
        'p_f': w(ks[14], (DEPTH, D_FOURIER, D_MODEL), D_FOURIER),
        'p_a': w(ks[15], (DEPTH, D_V, D_MODEL), D_V),
        'w_o': w(ks[16], (DEPTH, D_MODEL, D_MODEL), D_MODEL),
        'norm_ffb': gain(ks[17], D_MODEL),
        'ffb_gate': w(ks[18], (DEPTH, D_MODEL, D_FF), D_MODEL),
        'ffb_up': w(ks[19], (DEPTH, D_MODEL, D_FF), D_MODEL),
        'ffb_down': w(ks[20], (DEPTH, D_FF, D_MODEL), D_FF),
        'norm_out': gain(ks[21], D_MODEL),
    }


def reference(x, norm_ffa, ffa_gate, ffa_up, ffa_down, norm_mix, w_in, q_norm, k_norm,
              lambda_q1, lambda_k1, lambda_q2, lambda_k2, subln, p_f, p_a, w_o,
              norm_ffb, ffb_gate, ffb_up, ffb_down, norm_out):
    b, s, _ = x.shape
    cos, sin = rope_tables(s)
    splits = [D_FOURIER, D_FOURIER + D_QK, D_FOURIER + 2 * D_QK,
              D_FOURIER + 2 * D_QK + D_V, D_FOURIER + 2 * D_QK + D_V + D_MODEL]
    for i in range(DEPTH):
        lam_init = 0.8 - 0.6 * math.exp(-0.3 * i)
        x = x + 0.5 * swiglu(rms_norm(x, norm_ffa[i]), ffa_gate[i], ffa_up[i], ffa_down[i])
        h = rms_norm(x, norm_mix[i])
        z = h @ w_in[i]
        u_f, q, k, v, g_f, g_a = jnp.split(z, splits, axis=-1)
        f = fourier_mix(u_f)
        q = partial_rope(rms_norm(q.reshape(b, s, N_HEADS, 2, QK_HEAD_DIM), q_norm[i]), cos, sin)
        k = partial_rope(rms_norm(k.reshape(b, s, N_HEADS, 2, QK_HEAD_DIM), k_norm[i]), cos, sin)
        v = v.reshape(b, s, N_HEADS, V_HEAD_DIM)
        lam = (jnp.exp(jnp.sum(lambda_q1[i].astype(jnp.float32) * lambda_k1[i].astype(jnp.float32)))
               - jnp.exp(jnp.sum(lambda_q2[i].astype(jnp.float32) * lambda_k2[i].astype(jnp.float32)))
               + lam_init)
        o = diff_attention(q, k, v, lam)
        o = (rms_norm(o, subln[i]) * (1.0 - lam_init)).reshape(b, s, D_V)
        m = jax.nn.sigmoid(g_f) * (f @ p_f[i]) + jax.nn.sigmoid(g_a) * (o @ p_a[i])
        x = x + m @ w_o[i]
        x = x + 0.5 * swiglu(rms_norm(x, norm_ffb[i]), ffb_gate[i], ffb_up[i], ffb_down[i])
        x = rms_norm(x, norm_out[i])
    return x
```

```python
import math
from contextlib import ExitStack
import numpy as np
import ml_dtypes
import concourse.bass as bass
import concourse.mybir as mybir
from concourse.bass_utils import run_bass_kernel_spmd

F32 = mybir.dt.float32
BF16 = mybir.dt.bfloat16
ALU = mybir.AluOpType
AF = mybir.ActivationFunctionType

D = 2048
KC = 16
DFF = 5632
FC = 44
NH = 8
T = 512
EPS = 1e-6
ROPE_THETA = 500000.0


class Buf:
    __slots__ = ("w", "r")

    def __init__(self):
        self.w = None
        self.r = []


class BP:
    ENG = ("pe", "act", "dve", "pool", "sp")
    UID = 0
    NBLK = 0
    LVL = 9
    REV = False
    CNT = {}
    SEM = {}
    GST = None
    STOP = 10 ** 9

    def __init__(self, nc):
        self.nc = nc
        self.ops = {e: [] for e in self.ENG}
        self.cnt = BP.CNT
        self.sem = BP.SEM
        self.st = ExitStack()
        self.bufs = set()

    def _sem(self, name):
        if name not in self.sem:
            self.sem[name] = BP.GST.enter_context(self.nc.semaphore(name))
            self.cnt[name] = 0

    def op(self, eng, fn, reads=(), writes=(), dma=None, signal=True):
        deps = []
        for b in reads:
            if b.w is not None:
                deps.append(b.w)
        for b in writes:
            if b.w is not None:
                deps.append(b.w)
            deps.extend(b.r)
        if dma:
            semn, inc = "d_" + dma, 16
        else:
            semn, inc = "c_" + eng, 1
        self._sem(semn)
        if signal:
            self.cnt[semn] += inc
            ev = (semn, self.cnt[semn])
        else:
            ev = (semn, self.cnt[semn] + inc)
        if eng == "pe":
            deps = [d for d in deps if d[0] != "c_pe"]
        self.ops[eng].append((fn, deps, semn if signal else None, inc))
        for b in reads:
            b.r.append(ev)
            self.bufs.add(b)
        for b in writes:
            b.w = ev
            b.r = []
            self.bufs.add(b)
        return ev

    def run(self):
        nc = self.nc
        BP.NBLK += 1
        if BP.NBLK > BP.STOP:
            self.st.close()
            return
        final = dict(self.cnt)

        def mk(en):
            def body(e):
                waited = {}
                for fn, deps, semn, inc in self.ops[en]:
                    need = {}
                    for dn, dv in deps:
                        if dv > need.get(dn, 0):
                            need[dn] = dv
                    for dn, dv in need.items():
                        if waited.get(dn, 0) < dv:
                            e.wait_ge(self.sem[dn], dv)
                            waited[dn] = dv
                    ins = fn(e)
                    if semn is not None:
                        ins.then_inc(self.sem[semn], inc)
                if en == "sp":
                    for dn, dv in final.items():
                        if dv > 0 and waited.get(dn, 0) < dv:
                            e.wait_ge(self.sem[dn], dv)
            return body

        with nc.Block() as blk:
            blk.tensor(mk("pe"))
            blk.scalar(mk("act"))
            blk.vector(mk("dve"))
            blk.gpsimd(mk("pool"))
            blk.sync(mk("sp"))
        for b in self.bufs:
            b.w = None
            b.r = []
        self.st.close()


def lam_init_of(i):
    return 0.8 - 0.6 * math.exp(-0.3 * i)


DBG = False


def build_nc(S, DEPTH):
    N2 = S // 128
    G = 128 // N2
    NGRP = 128 // G
    NT = S // T
    NKT = S // 128
    nc = bass.Bass("TRN2", target_bir_lowering=False)
    BP.CNT = {}
    BP.SEM = {}
    BP.GST = ExitStack()
    BP.NBLK = 0

    def din(name, shape, dt=F32):
        return nc.dram_tensor(name, list(shape), dt, kind="ExternalInput").ap()

    def dscr(name, shape, dt=BF16):
        return nc.dram_tensor(name, list(shape), dt, kind="Internal").ap()

    xT = din("xT", [D, S])
    outT = nc.dram_tensor("outT", [D, S], F32, kind="ExternalOutput").ap()
    wgu_in = [din("wgu_a", [DEPTH, FC * 128, 2 * KC * 128]), din("wgu_b", [DEPTH, FC * 128, 2 * KC * 128])]
    wd_in = [din("wd_a", [DEPTH, KC * 128, FC * 128]), din("wd_b", [DEPTH, KC * 128, FC * 128])]
    win_in = din("win", [DEPTH, 64 * 128, KC * 128])
    pf_in = din("pf", [DEPTH, 16 * 128, 8 * 128])
    pa_in = din("pa", [DEPTH, 16 * 128, 8 * 128])
    wo_in = din("wo", [DEPTH, 16 * 128, 16 * 128])
    gains_in = din("gains", [DEPTH, 128, 4 * 16])
    small_in = din("small", [DEPTH, 128, 4])
    lamv_in = din("lamv", [DEPTH, 128, 256])
    c_ones = din("c_ones", [128, 128])
    c_blk = din("c_blk", [128, 128])
    c_rot = din("c_rot", [128, 128])
    c_cos = din("c_cos", [128, S])
    c_sin = din("c_sin", [128, S])
    c_cs = din("c_cs", [128, 2 * 512], BF16)
    c_m1 = din("c_m1", [128, 512], BF16)
    c_tw = din("c_tw", [128, 2 * N2 * 128], BF16)
    c_k3 = din("c_k3", [128, 256], BF16)
    c_onesb = din("c_onesb", [128, 128], BF16)
    c_ident = din("c_ident", [128, 128], BF16)

    xs = dscr("xs", [D, S], F32)
    wgu_bf = [dscr("wgu_bf0", [FC * 128, 2 * KC * 128]), dscr("wgu_bf1", [FC * 128, 2 * KC * 128])]
    wd_bf = [dscr("wd_bf0", [KC * 128, FC * 128]), dscr("wd_bf1", [KC * 128, FC * 128])]
    win_bf = dscr("win_bf", [64 * 128, KC * 128])
    pf_bf = dscr("pf_bf", [16 * 128, 8 * 128])
    pa_bf = dscr("pa_bf", [16 * 128, 8 * 128])
    wo_bf = dscr("wo_bf", [16 * 128, 16 * 128])
    qT_d = dscr("qT_d", [NH * 128, S])
    kT_d = dscr("kT_d", [NH * 128, S])
    v_d = dscr("v_d", [S, NH * 128])
    ab_d = dscr("ab_d", [S, 4 * 512])
    fT_d = dscr("fT_d", [8 * 128, S])
    oT_d = dscr("oT_d", [NH * 128, S])

    top = ExitStack()

    uid = [0]

    def sb(name, shape, dt=F32, st=top):
        uid[0] += 1
        return st.enter_context(nc.sbuf_tensor(f"s_{name}_{uid[0]}", list(shape), dt))

    PS = [top.enter_context(nc.psum_tensor(f"ps{i}", [128, 512], F32)) for i in range(7)]
    PSB = [Buf() for _ in range(7)]
    PTt = top.enter_context(nc.psum_tensor("pt", [128, 1024], BF16))
    PT = [PTt[:, 0:256], PTt[:, 256:512]]
    PTB = [Buf(), Buf()]

    ones_f = sb("ones_f", [128, 128]); blk_f = sb("blk_f", [128, 128]); rot_f = sb("rot_f", [128, 128])
    onesb = sb("onesb", [128, 128], BF16)
    identb = sb("identb", [128, 128], BF16)
    cs_b = sb("cs_b", [128, 2, 512], BF16)
    m1_b = sb("m1_b", [128, 2, 256], BF16)
    k3_b = sb("k3_b", [128, 2, 128], BF16)
    eps_t = sb("eps_t", [128, 1])
    gains = sb("gains", [128, 4, 16])
    small = sb("small", [128, 4])
    lamv = sb("lamv", [128, 4, 64])
    lamt = sb("lamt", [128, 8])
    CONST = Buf()
    LAYERC = Buf()

    bp = BP(nc)
    bp.op("sp", lambda e: e.dma_start(out=ones_f[:, :], in_=c_ones[:, :]), writes=[CONST], dma="c0")
    bp.op("sp", lambda e: e.dma_start(out=blk_f[:, :], in_=c_blk[:, :]), writes=[CONST], dma="c1")
    bp.op("sp", lambda e: e.dma_start(out=rot_f[:, :], in_=c_rot[:, :]), writes=[CONST], dma="c2")
    bp.op("sp", lambda e: e.dma_start(out=onesb[:, :], in_=c_onesb[:, :]), writes=[CONST], dma="c3")
    bp.op("sp", lambda e: e.dma_start(out=cs_b[:, :, :], in_=c_cs.rearrange("p (k c) -> p k c", k=2)), writes=[CONST], dma="c4")
    bp.op("sp", lambda e: e.dma_start(out=m1_b[:, :, :], in_=c_m1.rearrange("p (k c) -> p k c", k=2)), writes=[CONST], dma="c5")
    bp.op("sp", lambda e: e.dma_start(out=k3_b[:, :, :], in_=c_k3.rearrange("p (k c) -> p k c", k=2)), writes=[CONST], dma="c6")
    bp.op("sp", lambda e: e.dma_start(out=identb[:, :], in_=c_ident[:, :]), writes=[CONST], dma="c7")
    bp.op("dve", lambda e: e.memset(eps_t[:, :], EPS), writes=[CONST])
    for c in range(KC):
        bp.op("pool", lambda e, c=c: e.dma_start(out=xs[c * 128:(c + 1) * 128, :], in_=xT[c * 128:(c + 1) * 128, :]),
              writes=[Buf()], dma=f"x{c % 4}")
    bp.run()

    def cast_rows(bp, dst, src, nrows, tag):
        for i in range(nrows // 128):
            bp.op("pool", lambda e, i=i: e.dma_start(out=dst[i * 128:(i + 1) * 128, :], in_=src[i * 128:(i + 1) * 128, :]),
                  writes=[Buf()], dma=f"cast{i % 4}")

    def rmsnorm(bp, xt, XT, g_idx, tmp, dst_fn, DST, sqb, SQ):
        rt, rstd, RT, RSTD = tmp
        for c in range(KC):
            s = c % 4
            bp.op("act", lambda e, c=c, s=s: e.activation(out=sqb[:, s, :], in_=xt[:, c, :], func=AF.Square),
                  reads=[XT[c]], writes=[SQ[s]])
            bp.op("pe", lambda e, c=c, s=s: e.matmul(PS[6][:, :], lhsT=ones_f[:, :], rhs=sqb[:, s, :],
                                                    start=(c == 0), stop=(c == KC - 1)),
                  reads=[SQ[s], CONST], writes=[PSB[6]])
        bp.op("act", lambda e: e.activation(out=rt[:, :], in_=PS[6][:, :], func=AF.Sqrt, bias=eps_t[:, 0:1], scale=1.0 / D),
              reads=[PSB[6], CONST], writes=[RT])
        bp.op("dve", lambda e: e.reciprocal(out=rstd[:, :], in_=rt[:, :]), reads=[RT], writes=[RSTD])
        for c in range(KC):
            bp.op("dve", lambda e, c=c: e.scalar_tensor_tensor(out=dst_fn(c), in0=xt[:, c, :], scalar=gains[:, g_idx, c:c + 1],
                                                              in1=rstd[:, :], op0=ALU.mult, op1=ALU.mult),
                  reads=[XT[c], RSTD, LAYERC], writes=[DST[c]])

    def ffn(bp, xt, XT, ht, HT, at, AT, wgu, WGU, wd, WD, sg, SG, wgu_src, wd_src):
        for fc in range(FC):
            s = fc % 2
            bp.op("sp", lambda e, fc=fc, s=s: e.dma_start(
                out=wgu[s][:, :, :, :], in_=wgu_src[fc * 128:(fc + 1) * 128, :].rearrange("p (g k f) -> p g k f", g=2, k=KC)),
                writes=[WGU[s]], dma=f"wgu{s}")
            pg, pu = 2 * s, 2 * s + 1
            for kc in range(KC):
                bp.op("pe", lambda e, kc=kc, s=s, pg=pg: e.matmul(PS[pg][:, :], lhsT=wgu[s][:, 0, kc, :], rhs=ht[:, kc, :],
                                                                 start=(kc == 0), stop=(kc == KC - 1)),
                      reads=[WGU[s], HT[kc]], writes=[PSB[pg]], signal=(kc == KC - 1))
            for kc in range(KC):
                bp.op("pe", lambda e, kc=kc, s=s, pu=pu: e.matmul(PS[pu][:, :], lhsT=wgu[s][:, 1, kc, :], rhs=ht[:, kc, :],
                                                                 start=(kc == 0), stop=(kc == KC - 1)),
                      reads=[WGU[s], HT[kc]], writes=[PSB[pu]], signal=(kc == KC - 1))
            bp.op("act", lambda e, s=s, pg=pg: e.activation(out=sg[s][:, :], in_=PS[pg][:, :], func=AF.Silu),
                  reads=[PSB[pg]], writes=[SG[s]])
            bp.op("dve", lambda e, fc=fc, s=s, pu=pu: e.tensor_tensor(out=at[:, fc, :], in0=sg[s][:, :], in1=PS[pu][:, :], op=ALU.mult),
                  reads=[SG[s], PSB[pu]], writes=[AT[fc]])
        for dc in range(KC):
            s = dc % 2
            bp.op("sp", lambda e, dc=dc, s=s: e.dma_start(
                out=wd[s][:, :, :], in_=wd_src[dc * 128:(dc + 1) * 128, :].rearrange("p (k f) -> p k f", k=FC)),
                writes=[WD[s]], dma=f"wd{s}")
            py = 4 + s
            for fc in range(FC):
                bp.op("pe", lambda e, fc=fc, s=s, py=py: e.matmul(PS[py][:, :], lhsT=wd[s][:, fc, :], rhs=at[:, fc, :],
                                                                 start=(fc == 0), stop=(fc == FC - 1)),
                      reads=[WD[s], AT[fc]], writes=[PSB[py]], signal=(fc == FC - 1))
            bp.op("dve", lambda e, dc=dc, py=py: e.scalar_tensor_tensor(out=xt[:, dc, :], in0=PS[py][:, :], scalar=0.5,
                                                                       in1=xt[:, dc, :], op0=ALU.mult, op1=ALU.add),
                  reads=[PSB[py], XT[dc]], writes=[XT[dc]])

    def proj(bp, ws, WS, slot, src_rows, nk, rhs_fn, RHS, pbank, tag):
        bp.op("sp", lambda e: e.dma_start(out=ws[slot][:, 0:nk, :], in_=src_rows.rearrange("p (k f) -> p k f", k=nk)),
              writes=[WS[slot]], dma=f"{tag}{slot}")
        for kc in range(nk):
            bp.op("pe", lambda e, kc=kc: e.matmul(PS[pbank][:, :], lhsT=ws[slot][:, kc, :], rhs=rhs_fn(kc),
                                                  start=(kc == 0), stop=(kc == nk - 1)),
                  reads=[WS[slot], RHS[kc]], writes=[PSB[pbank]], signal=(kc == nk - 1))

    for l in range(DEPTH):
        li = lam_init_of(l)
        bp = BP(nc)
        for ab in range(2):
            cast_rows(bp, wgu_bf[ab], wgu_in[ab][l], FC * 128, f"g{ab}")
            cast_rows(bp, wd_bf[ab], wd_in[ab][l], KC * 128, f"d{ab}")
        cast_rows(bp, win_bf, win_in[l], 64 * 128, "wi")
        cast_rows(bp, pf_bf, pf_in[l], 16 * 128, "pf")
        cast_rows(bp, pa_bf, pa_in[l], 16 * 128, "pa")
        cast_rows(bp, wo_bf, wo_in[l], 16 * 128, "wo")
        bp.op("sp", lambda e, l=l: e.dma_start(out=gains[:, :, :], in_=gains_in[l].rearrange("p (g c) -> p g c", g=4)),
              writes=[LAYERC], dma="l0")
        bp.op("sp", lambda e, l=l: e.dma_start(out=small[:, :], in_=small_in[l]), writes=[LAYERC], dma="l1")
        LV = Buf()
        bp.op("sp", lambda e, l=l: e.dma_start(out=lamv[:, :, :], in_=lamv_in[l].rearrange("p (g c) -> p g c", g=4)),
              writes=[LV], dma="l2")
        LT = Buf()
        bp.op("dve", lambda e: e.tensor_tensor(out=lamv[:, 0, :], in0=lamv[:, 0, :], in1=lamv[:, 1, :], op=ALU.mult), reads=[LV], writes=[LV])
        bp.op("dve", lambda e: e.tensor_tensor(out=lamv[:, 2, :], in0=lamv[:, 2, :], in1=lamv[:, 3, :], op=ALU.mult), reads=[LV], writes=[LV])
        bp.op("dve", lambda e: e.tensor_reduce(out=lamt[:, 0:1], in_=lamv[:, 0, :], axis=mybir.AxisListType.X, op=ALU.add), reads=[LV], writes=[LT])
        bp.op("dve", lambda e: e.tensor_reduce(out=lamt[:, 1:2], in_=lamv[:, 2, :], axis=mybir.AxisListType.X, op=ALU.add), reads=[LT, LV], writes=[LT])
        bp.op("act", lambda e: e.activation(out=lamt[:, 2:4], in_=lamt[:, 0:2], func=AF.Exp), reads=[LT], writes=[LT])
        bp.op("dve", lambda e, li=li: e.scalar_tensor_tensor(out=lamt[:, 4:5], in0=lamt[:, 2:3], scalar=li, in1=lamt[:, 3:4],
                                                             op0=ALU.add, op1=ALU.subtract), reads=[LT], writes=[LT])
        bp.op("dve", lambda e: e.tensor_scalar(out=lamt[:, 5:6], in0=lamt[:, 4:5], scalar1=-1.0, scalar2=None, op0=ALU.mult),
              reads=[LT], writes=[LT])
        bp.op("dve", lambda e, li=li: e.tensor_scalar(out=small[:, 3:4], in0=small[:, 2:3], scalar1=(1.0 - li), scalar2=None, op0=ALU.mult),
              reads=[LAYERC], writes=[LAYERC])
        bp.run()

        for phase in ("A", "C"):
            st = ExitStack()
            xt = sb("xt", [128, KC, T], F32, st)
            ht = sb("ht", [128, KC, T], BF16, st)
            big = sb("big", [128, FC * T], BF16, st)
            at = big[:, :].rearrange("p (k t) -> p k t", k=FC)
            wgu = [sb(f"wgu{i}", [128, 2, KC, 128], BF16, st) for i in range(2)]
            wd = [sb(f"wd{i}", [128, FC, 128], BF16, st) for i in range(2)]
            ws = [sb(f"ws{i}", [128, KC, 128], BF16, st) for i in range(2)]
            sqb = sb("sqb", [128, 4, T], F32, st)
            sg = [sb(f"sg{i}", [128, T], F32, st) for i in range(2)]
            tmp = [sb(f"tmp{i}", [128, T], F32, st) for i in range(8)]
            qo = [sb(f"qo{i}", [128, T], BF16, st) for i in range(2)]
            cst = sb("cst", [128, 2, T], F32, st)
            XT = [Buf() for _ in range(KC)]; HT = [Buf() for _ in range(KC)]; AT = [Buf() for _ in range(FC)]
            WGU = [Buf(), Buf()]; WD = [Buf(), Buf()]; WS = [Buf(), Buf()]; SQ = [Buf() for _ in range(4)]
            SG = [Buf(), Buf()]; TMP = [Buf() for _ in range(8)]; QO = [Buf(), Buf()]; CST = Buf()
            ut = big[:, 0:8 * T].rearrange("p (k t) -> p k t", k=8)
            vo = big[:, 8 * T:16 * T].rearrange("p (s c) -> p s c", s=4)
            abo = big[:, 16 * T:32 * T].rearrange("p (s g c) -> p s g c", s=4, g=4)
            UT = [Buf() for _ in range(8)]; VO = Buf(); ABO = Buf()
            ftt = big[:, 0:8 * T].rearrange("p (k t) -> p k t", k=8)
            ott = big[:, 8 * T:16 * T].rearrange("p (k t) -> p k t", k=8)
            mt = big[:, 16 * T:32 * T].rearrange("p (k t) -> p k t", k=16)
            FTT = [Buf() for _ in range(8)]; OTT = [Buf() for _ in range(8)]; MT = [Buf() for _ in range(16)]

            for ti in (range(NT) if (phase == "A" or not BP.REV) else reversed(range(NT))):
                t0 = ti * T
                bp = BP(nc)
                bp.op("sp", lambda e: e.dma_start(out=xt[:, :, :], in_=xs[:, t0:t0 + T].rearrange("(c p) t -> p c t", p=128)),
                      writes=XT, dma="xt")
                norm_tmp = (tmp[0], tmp[1], TMP[0], TMP[1])
                if phase == "A":
                    rmsnorm(bp, xt, XT, 0, norm_tmp, lambda c: ht[:, c, :], HT, sqb, SQ)
                    ffn(bp, xt, XT, ht, HT, at, AT, wgu, WGU, wd, WD, sg, SG, wgu_bf[0], wd_bf[0])
                    bp.op("pool", lambda e: e.dma_start(out=xs[:, t0:t0 + T].rearrange("(c p) t -> p c t", p=128), in_=xt[:, :, :]),
                          reads=XT, dma="xo")
                    rmsnorm(bp, xt, XT, 1, norm_tmp, lambda c: ht[:, c, :], HT, sqb, SQ)
                    bp.op("sp", lambda e: e.dma_start(out=cst[:, 0, :], in_=c_cos[:, t0:t0 + T]), writes=[CST], dma="cs0")
                    bp.op("sp", lambda e: e.dma_start(out=cst[:, 1, :], in_=c_sin[:, t0:t0 + T]), writes=[CST], dma="cs1")
                    n = 0
                    for oc in range(8):
                        s = n % 2; pb = n % 2; n += 1
                        proj(bp, ws, WS, s, win_bf[oc * 128:(oc + 1) * 128, :], KC, lambda kc: ht[:, kc, :], HT, pb, "ws")
                        bp.op("act", lambda e, oc=oc, pb=pb: e.activation(out=ut[:, oc, :], in_=PS[pb][:, :], func=AF.Copy),
                              reads=[PSB[pb]], writes=[UT[oc]])
                    for oc in range(8, 24):
                        s = n % 2; pb = n % 2; n += 1
                        isq = oc < 16
                        hd = oc - 8 if isq else oc - 16
                        dstd = qT_d if isq else kT_d
                        gcol = 0 if isq else 1
                        proj(bp, ws, WS, s, win_bf[oc * 128:(oc + 1) * 128, :], KC, lambda kc: ht[:, kc, :], HT, pb, "ws")
                        bp.op("act", lambda e, pb=pb: e.activation(out=tmp[2][:, :], in_=PS[pb][:, :], func=AF.Square),
                              reads=[PSB[pb]], writes=[TMP[2]])
                        bp.op("pe", lambda e: e.matmul(PS[2][:, :], lhsT=blk_f[:, :], rhs=tmp[2][:, :], start=True, stop=True),
                              reads=[TMP[2], CONST], writes=[PSB[2]])
                        bp.op("act", lambda e: e.activation(out=tmp[3][:, :], in_=PS[2][:, :], func=AF.Sqrt, bias=eps_t[:, 0:1], scale=1.0 / 64),
                              reads=[PSB[2]], writes=[TMP[3]])
                        bp.op("dve", lambda e: e.reciprocal(out=tmp[4][:, :], in_=tmp[3][:, :]), reads=[TMP[3]], writes=[TMP[4]])
                        bp.op("dve", lambda e, pb=pb, gcol=gcol: e.scalar_tensor_tensor(
                            out=tmp[5][:, :], in0=PS[pb][:, :], scalar=small[:, gcol:gcol + 1], in1=tmp[4][:, :], op0=ALU.mult, op1=ALU.mult),
                            reads=[PSB[pb], TMP[4], LAYERC], writes=[TMP[5]])
                        bp.op("pe", lambda e: e.matmul(PS[3][:, :], lhsT=rot_f[:, :], rhs=tmp[5][:, :], start=True, stop=True),
                              reads=[TMP[5], CONST], writes=[PSB[3]])
                        bp.op("dve", lambda e: e.tensor_tensor(out=tmp[6][:, :], in0=tmp[5][:, :], in1=cst[:, 0, :], op=ALU.mult),
                              reads=[TMP[5], CST], writes=[TMP[6]])
                        bp.op("dve", lambda e: e.tensor_tensor(out=tmp[7][:, :], in0=PS[3][:, :], in1=cst[:, 1, :], op=ALU.mult),
                              reads=[PSB[3], CST], writes=[TMP[7]])
                        q = hd % 2
                        bp.op("pool", lambda e, q=q: e.tensor_tensor(out=qo[q][:, :], in0=tmp[6][:, :], in1=tmp[7][:, :], op=ALU.add),
                              reads=[TMP[6], TMP[7]], writes=[QO[q]])
                        bp.op("pool", lambda e, q=q, hd=hd, dstd=dstd: e.dma_start(out=dstd[hd * 128:(hd + 1) * 128, t0:t0 + T], in_=qo[q][:, :]),
                              reads=[QO[q]], dma=f"qo{q}")
                    for oc in range(24, 32):
                        s = n % 2; n += 1
                        bp.op("sp", lambda e, oc=oc, s=s: e.dma_start(
                            out=ws[s][:, :, :], in_=win_bf[oc * 128:(oc + 1) * 128, :].rearrange("p (k f) -> p k f", k=KC)),
                            writes=[WS[s]], dma=f"ws{s}")
                        for sub in range(4):
                            pb = 4 + (sub % 2)
                            for kc in range(KC):
                                bp.op("pe", lambda e, kc=kc, sub=sub, s=s, pb=pb: e.matmul(
                                    PS[pb][:, 0:128], lhsT=ht[:, kc, sub * 128:(sub + 1) * 128], rhs=ws[s][:, kc, :],
                                    start=(kc == 0), stop=(kc == KC - 1)),
                                    reads=[WS[s], HT[kc]], writes=[PSB[pb]], signal=(kc == KC - 1))
                            bp.op("act", lambda e, sub=sub, oc=oc, pb=pb: e.activation(
                                out=vo[:, sub, (oc - 24) * 128:(oc - 23) * 128], in_=PS[pb][:, 0:128], func=AF.Copy),
                                reads=[PSB[pb]], writes=[VO])
                    bp.op("pool", lambda e: e.dma_start(out=v_d[t0:t0 + T, :].rearrange("(s p) c -> p s c", p=128), in_=vo),
                          reads=[VO], dma="vo")
                    for sub in range(4):
                        for g in range(4):
                            pb = 4 + (g % 2)
                            for kc in range(2):
                                bp.op("pe", lambda e, kc=kc, sub=sub, g=g, pb=pb: e.matmul(
                                    PS[pb][:, :], lhsT=ut[:, 2 * g + kc, sub * 128:(sub + 1) * 128], rhs=cs_b[:, kc, :],
                                    start=(kc == 0), stop=(kc == 1)),
                                    reads=[UT[2 * g + kc], CONST], writes=[PSB[pb]], signal=(kc == 1))
                            bp.op("dve", lambda e, sub=sub, g=g, pb=pb: e.tensor_copy(out=abo[:, sub, g, :], in_=PS[pb][:, :]),
                                  reads=[PSB[pb]], writes=[ABO])
                    bp.op("pool", lambda e: e.dma_start(out=ab_d[t0:t0 + T, :].rearrange("(s p) c -> p s c", p=128),
                                                        in_=abo.rearrange("p s g c -> p s (g c)")),
                          reads=[ABO], dma="abo")
                else:
                    rmsnorm(bp, xt, XT, 1, norm_tmp, lambda c: ht[:, c, :], HT, sqb, SQ)
                    bp.op("sp", lambda e: e.dma_start(out=ftt, in_=fT_d[:, t0:t0 + T].rearrange("(c p) t -> p c t", p=128)),
                          writes=FTT, dma="ft")
                    bp.op("sp", lambda e: e.dma_start(out=ott, in_=oT_d[:, t0:t0 + T].rearrange("(c p) t -> p c t", p=128)),
                          writes=OTT, dma="ot")
                    for dc in range(16):
                        proj(bp, ws, WS, 0, win_bf[(32 + dc) * 128:(33 + dc) * 128, :], KC, lambda kc: ht[:, kc, :], HT, 0, "ws")
                        bp.op("act", lambda e: e.activation(out=tmp[2][:, :], in_=PS[0][:, :], func=AF.Sigmoid), reads=[PSB[0]], writes=[TMP[2]])
                        proj(bp, ws, WS, 1, win_bf[(48 + dc) * 128:(49 + dc) * 128, :], KC, lambda kc: ht[:, kc, :], HT, 1, "ws")
                        bp.op("act", lambda e: e.activation(out=tmp[3][:, :], in_=PS[1][:, :], func=AF.Sigmoid), reads=[PSB[1]], writes=[TMP[3]])
                        proj(bp, ws, WS, 0, pf_bf[dc * 128:(dc + 1) * 128, :], 8, lambda kc: ftt[:, kc, :], FTT, 2, "ws")
                        proj(bp, ws, WS, 1, pa_bf[dc * 128:(dc + 1) * 128, :], 8, lambda kc: ott[:, kc, :], OTT, 3, "ws")
                        bp.op("dve", lambda e: e.tensor_tensor(out=tmp[4][:, :], in0=tmp[2][:, :], in1=PS[2][:, :], op=ALU.mult),
                              reads=[TMP[2], PSB[2]], writes=[TMP[4]])
                        bp.op("dve", lambda e: e.tensor_tensor(out=tmp[5][:, :], in0=tmp[3][:, :], in1=PS[3][:, :], op=ALU.mult),
                              reads=[TMP[3], PSB[3]], writes=[TMP[5]])
                        bp.op("pool", lambda e, dc=dc: e.tensor_tensor(out=mt[:, dc, :], in0=tmp[4][:, :], in1=tmp[5][:, :], op=ALU.add),
                              reads=[TMP[4], TMP[5]], writes=[MT[dc]])
                    for dc in range(16):
                        s = dc % 2
                        proj(bp, ws, WS, s, wo_bf[dc * 128:(dc + 1) * 128, :], KC, lambda kc: mt[:, kc, :], MT, 4 + s, "ws")
                        bp.op("dve", lambda e, dc=dc, s=s: e.tensor_tensor(out=xt[:, dc, :], in0=xt[:, dc, :], in1=PS[4 + s][:, :], op=ALU.add),
                              reads=[PSB[4 + s], XT[dc]], writes=[XT[dc]])
                    rmsnorm(bp, xt, XT, 2, norm_tmp, lambda c: ht[:, c, :], HT, sqb, SQ)
                    ffn(bp, xt, XT, ht, HT, at, AT, wgu, WGU, wd, WD, sg, SG, wgu_bf[1], wd_bf[1])
                    rmsnorm(bp, xt, XT, 3, norm_tmp, lambda c: xt[:, c, :], XT, sqb, SQ)
                    dst = outT if l == DEPTH - 1 else xs
                    bp.op("pool", lambda e, dst=dst: e.dma_start(out=dst[:, t0:t0 + T].rearrange("(c p) t -> p c t", p=128), in_=xt[:, :, :]),
                          reads=XT, dma="xo")
                bp.run()
            st.close()

            if phase == "C":
                continue
            st = ExitStack()
            qs_ = sb("qs_", [128, S], BF16, st); ks_ = sb("ks_", [128, S], BF16, st)
            vs_ = sb("vs_", [128, NKT, 128], BF16, st)
            pr = [sb(f"pr{i}", [128, T], BF16, st) for i in range(3)]
            ft = [sb(f"ft{i}", [128, T], F32, st) for i in range(6)]
            oo = [sb(f"oo{i}", [128, T], BF16, st) for i in range(2)]
            QS = Buf(); KS = Buf(); VS = Buf(); PR = [Buf() for _ in range(4)]; FTB = [Buf() for _ in range(6)]; OO = [Buf(), Buf()]
            for h in range(NH):
                bp = BP(nc)
                bp.op("sp", lambda e, h=h: e.dma_start(out=qs_[:, :], in_=qT_d[h * 128:(h + 1) * 128, :]), writes=[QS], dma="q")
                bp.op("sp", lambda e, h=h: e.dma_start(out=ks_[:, :], in_=kT_d[h * 128:(h + 1) * 128, :]), writes=[KS], dma="k")
                bp.op("sp", lambda e, h=h: e.dma_start(out=vs_[:, :, :], in_=v_d[:, h * 128:(h + 1) * 128].rearrange("(k p) c -> p k c", p=128)),
                      writes=[VS], dma="v")
                for qc in range(NT):
                    q0 = qc * T
                    for kt in range(NKT):
                        for cp in range(2):
                            pb = (2 * kt + cp) % 3
                            r = (2 * kt + cp) % 3
                            bp.op("pe", lambda e, kt=kt, cp=cp, pb=pb, q0=q0: e.matmul(
                                PS[pb][:, :], lhsT=ks_[cp * 64:(cp + 1) * 64, kt * 128:(kt + 1) * 128],
                                rhs=qs_[cp * 64:(cp + 1) * 64, q0:q0 + T], start=True, stop=True),
                                reads=[KS, QS], writes=[PSB[pb]])
                            bp.op("act", lambda e, pb=pb, r=r: e.activation(out=pr[r][:, :], in_=PS[pb][:, :], func=AF.Exp, scale=0.125),
                                  reads=[PSB[pb]], writes=[PR[r]])
                        for cp in range(2):
                            r = (2 * kt + cp) % 3
                            bp.op("pe", lambda e, kt=kt, cp=cp, r=r: e.matmul(
                                PS[3 + cp][:, :], lhsT=vs_[:, kt, :], rhs=pr[r][:, :], start=(kt == 0), stop=(kt == NKT - 1)),
                                reads=[VS, PR[r]], writes=[PSB[3 + cp]], signal=False)
                            bp.op("pe", lambda e, kt=kt, cp=cp, r=r: e.matmul(
                                PS[5 + cp][:, :], lhsT=onesb[:, :], rhs=pr[r][:, :], start=(kt == 0), stop=(kt == NKT - 1)),
                                reads=[CONST, PR[r]], writes=[PSB[5 + cp]])
                    bp.op("dve", lambda e: e.reciprocal(out=ft[0][:, :], in_=PS[5][:, :]), reads=[PSB[5]], writes=[FTB[0]])
                    bp.op("dve", lambda e: e.reciprocal(out=ft[1][:, :], in_=PS[6][:, :]), reads=[PSB[6]], writes=[FTB[1]])
                    bp.op("dve", lambda e: e.tensor_tensor(out=ft[2][:, :], in0=ft[0][:, :], in1=PS[3][:, :], op=ALU.mult),
                          reads=[FTB[0], PSB[3]], writes=[FTB[2]])
                    bp.op("dve", lambda e: e.tensor_tensor(out=ft[3][:, :], in0=ft[1][:, :], in1=PS[4][:, :], op=ALU.mult),
                          reads=[FTB[1], PSB[4]], writes=[FTB[3]])
                    bp.op("dve", lambda e: e.scalar_tensor_tensor(out=ft[4][:, :], in0=ft[3][:, :], scalar=lamt[:, 5:6], in1=ft[2][:, :],
                                                                  op0=ALU.mult, op1=ALU.add),
                          reads=[FTB[3], FTB[2]], writes=[FTB[4]])
                    bp.op("act", lambda e: e.activation(out=ft[5][:, :], in_=ft[4][:, :], func=AF.Square), reads=[FTB[4]], writes=[FTB[5]])
                    bp.op("pe", lambda e: e.matmul(PS[0][:, :], lhsT=ones_f[:, :], rhs=ft[5][:, :], start=True, stop=True),
                          reads=[FTB[5], CONST], writes=[PSB[0]])
                    bp.op("act", lambda e: e.activation(out=ft[0][:, :], in_=PS[0][:, :], func=AF.Sqrt, bias=eps_t[:, 0:1], scale=1.0 / 128),
                          reads=[PSB[0]], writes=[FTB[0]])
                    bp.op("dve", lambda e: e.reciprocal(out=ft[1][:, :], in_=ft[0][:, :]), reads=[FTB[0]], writes=[FTB[1]])
                    o = qc % 2
                    bp.op("dve", lambda e, o=o: e.scalar_tensor_tensor(out=oo[o][:, :], in0=ft[4][:, :], scalar=small[:, 3:4], in1=ft[1][:, :],
                                                                       op0=ALU.mult, op1=ALU.mult),
                          reads=[FTB[4], FTB[1]], writes=[OO[o]])
                    bp.op("pool", lambda e, o=o, h=h, q0=q0: e.dma_start(out=oT_d[h * 128:(h + 1) * 128, q0:q0 + T], in_=oo[o][:, :]),
                          reads=[OO[o]], dma=f"oo{o}")
                bp.run()
            st.close()

            st = ExitStack()
            xsb = sb("xsb", [128, N2, 256], BF16, st)
            tw = sb("tw", [128, 2, N2, 128], BF16, st)
            yp = sb("yp", [128, 256, N2], BF16, st)
            fts = sb("fts", [128, S], BF16, st)
            fq = [sb(f"fq{i}", [128, 256], F32, st) for i in range(4)]
            yt = [sb(f"yt{i}", [128, 2, 128], BF16, st) for i in range(2)]
            XSB = Buf(); TW = Buf(); YP = [Buf() for _ in range(N2)]; FTS = Buf(); FQ = [Buf() for _ in range(4)]; YT = [Buf(), Buf()]
            for j in range(8):
                g, hf = j // 2, j % 2
                bp = BP(nc)
                bp.op("sp", lambda e: e.dma_start(out=tw[:, :, :, :], in_=c_tw.rearrange("p (a s t) -> p a s t", a=2, s=N2)), writes=[TW], dma="tw")
                bp.op("sp", lambda e, g=g, hf=hf: e.dma_start(
                    out=xsb[:, :, :], in_=ab_d[:, g * 512 + hf * 256: g * 512 + hf * 256 + 256].rearrange("(p s) c -> p s c", s=N2)),
                    writes=[XSB], dma="xsb")
                for s2 in range(N2 if BP.LVL >= 2 else 0):
                    pb = s2 % 2
                    bp.op("pe", lambda e, s2=s2, pb=pb: e.matmul(PS[pb][:, 0:256], lhsT=xsb[:, s2, 0:128], rhs=m1_b[:, 0, :], start=True, stop=False),
                          reads=[XSB, CONST], writes=[PSB[pb]], signal=False)
                    bp.op("pe", lambda e, s2=s2, pb=pb: e.matmul(PS[pb][:, 0:256], lhsT=xsb[:, s2, 128:256], rhs=m1_b[:, 1, :], start=False, stop=True),
                          reads=[XSB, CONST], writes=[PSB[pb]])
                    a, b = 2 * (s2 % 2), 2 * (s2 % 2) + 1
                    bp.op("dve", lambda e, s2=s2, pb=pb, a=a: e.tensor_tensor(out=fq[a][:, 0:128], in0=PS[pb][:, 0:128], in1=tw[:, 0, s2, :], op=ALU.mult),
                          reads=[PSB[pb], TW], writes=[FQ[a]])
                    bp.op("dve", lambda e, s2=s2, pb=pb, a=a: e.tensor_tensor(out=fq[a][:, 128:256], in0=PS[pb][:, 128:256], in1=tw[:, 0, s2, :], op=ALU.mult),
                          reads=[PSB[pb], TW, FQ[a]], writes=[FQ[a]])
                    bp.op("dve", lambda e, s2=s2, pb=pb, b=b: e.tensor_tensor(out=fq[b][:, 0:128], in0=PS[pb][:, 128:256], in1=tw[:, 1, s2, :], op=ALU.mult),
                          reads=[PSB[pb], TW], writes=[FQ[b]])
                    bp.op("dve", lambda e, s2=s2, pb=pb, b=b: e.tensor_tensor(out=fq[b][:, 128:256], in0=PS[pb][:, 0:128], in1=tw[:, 1, s2, :], op=ALU.mult),
                          reads=[PSB[pb], TW, FQ[b]], writes=[FQ[b]])
                    bp.op("pool", lambda e, s2=s2, a=a, b=b: e.tensor_tensor(out=yp[:, 0:128, s2], in0=fq[a][:, 0:128], in1=fq[b][:, 0:128], op=ALU.add),
                          reads=[FQ[a], FQ[b]], writes=[YP[s2]])
                    bp.op("pool", lambda e, s2=s2, a=a, b=b: e.tensor_tensor(out=yp[:, 128:256, s2], in0=fq[a][:, 128:256], in1=fq[b][:, 128:256], op=ALU.subtract),
                          reads=[FQ[a], FQ[b], YP[s2]], writes=[YP[s2]])
                for gp in range(NGRP if BP.LVL >= 3 else 0):
                    y = gp % 2
                    for ri in range(2):
                        src = yp[:, ri * 128 + gp * G: ri * 128 + gp * G + G, :].rearrange("p t s -> p (t s)")
                        bp.op("pe", lambda e, src=src, ri=ri, y=y: e.transpose(out=PT[y][:, ri * 128:(ri + 1) * 128], in_=src, identity=identb[:, :]),
                              reads=YP + [CONST], writes=[PTB[y]])
                    if BP.LVL < 4:
                        continue
                    bp.op("act", lambda e, y=y: e.activation(out=yt[y][:, :, :].rearrange("p a c -> p (a c)"), in_=PT[y], func=AF.Copy),
                          reads=[PTB[y]], writes=[YT[y]])
                    pb = 4 + y
                    bp.op("pe", lambda e, y=y, pb=pb: e.matmul(PS[pb][:, 0:128], lhsT=yt[y][:, 0, :], rhs=k3_b[:, 0, :], start=True, stop=False),
                          reads=[YT[y], CONST], writes=[PSB[pb]], signal=False)
                    bp.op("pe", lambda e, y=y, pb=pb: e.matmul(PS[pb][:, 0:128], lhsT=yt[y][:, 1, :], rhs=k3_b[:, 1, :], start=False, stop=True),
                          reads=[YT[y], CONST], writes=[PSB[pb]])
                    dstv = fts[:, :].rearrange("p (t2 t1) -> p t1 t2", t1=128)[:, gp * G:(gp + 1) * G, :]
                    bp.op("dve", lambda e, pb=pb, dstv=dstv: e.tensor_copy(out=dstv, in_=PS[pb][:, 0:128].rearrange("p (a b) -> p a b", a=G)),
                          reads=[PSB[pb]], writes=[FTS])
                bp.op("pool", lambda e, j=j: e.dma_start(out=fT_d[j * 128:(j + 1) * 128, :], in_=fts[:, :]), reads=[FTS], dma="fts")
                bp.run()
            st.close()
            if DBG and l == 0:
                bp = BP(nc)
                for nm, src, dt in (("q", qT_d, BF16), ("k", kT_d, BF16), ("o", oT_d, BF16), ("f", fT_d, BF16)):
                    dd = nc.dram_tensor("dbg_" + nm, [1024, S], dt, kind="ExternalOutput").ap()
                    bp.op("pool", lambda e, dd=dd, src=src: e.dma_start(out=dd[:, :], in_=src[:, :]), writes=[Buf()], dma="dbg" + nm)
                dd = nc.dram_tensor("dbg_x1", [D, S], F32, kind="ExternalOutput").ap()
                bp.op("pool", lambda e, dd=dd: e.dma_start(out=dd[:, :], in_=xs[:, :]), writes=[Buf()], dma="dbgx")
                dd = nc.dram_tensor("dbg_v", [S, 1024], BF16, kind="ExternalOutput").ap()
                bp.op("pool", lambda e, dd=dd: e.dma_start(out=dd[:, :], in_=v_d[:, :]), writes=[Buf()], dma="dbgv")
                dd = nc.dram_tensor("dbg_ab", [S, 2048], BF16, kind="ExternalOutput").ap()
                bp.op("pool", lambda e, dd=dd: e.dma_start(out=dd[:, :], in_=ab_d[:, :]), writes=[Buf()], dma="dbgab")
                bp.run()
    top.close()
    BP.GST.close()
    return nc


def _consts(S):
    N2 = S // 128
    G = 128 // N2
    bf = ml_dtypes.bfloat16
    c = {}
    c["c_ones"] = np.ones((128, 128), np.float32)
    blk = np.zeros((128, 128), np.float32)
    blk[:64, :64] = 1.0
    blk[64:, 64:] = 1.0
    c["c_blk"] = blk
    rot = np.zeros((128, 128), np.float32)
    for base in (0, 64):
        for i in range(8):
            rot[base + i + 8, base + i] = -1.0
            rot[base + i, base + i + 8] = 1.0
    c["c_rot"] = rot
    pos = np.arange(S, dtype=np.float32)
    inv = (np.float32(ROPE_THETA) ** (-(np.arange(0, 16, 2, dtype=np.float32)) / np.float32(16))).astype(np.float32)
    ang = (pos[None, :] * inv[:, None]).astype(np.float32)
    cosT = np.ones((128, S), np.float32)
    sinT = np.zeros((128, S), np.float32)
    for p in range(128):
        d = p % 64
        if d < 16:
            cosT[p] = np.cos(ang[d % 8])
            sinT[p] = np.sin(ang[d % 8])
    c["c_cos"] = cosT
    c["c_sin"] = sinT
    cc = np.arange(256)[:, None].astype(np.float64)
    cp = np.arange(256)[None, :].astype(np.float64)
    angc = 2 * np.pi * cc * cp / 256.0
    Cc = np.cos(angc) / 16.0
    Sc = -np.sin(angc) / 16.0
    cs = np.zeros((256, 512))
    for hf in range(2):
        cs[:, hf * 256:hf * 256 + 128] = Cc[:, hf * 128:(hf + 1) * 128]
        cs[:, hf * 256 + 128:hf * 256 + 256] = Sc[:, hf * 128:(hf + 1) * 128]
    c["c_cs"] = np.ascontiguousarray(cs.reshape(2, 128, 512).transpose(1, 0, 2).reshape(128, 1024)).astype(bf)
    a1 = 2 * np.pi * np.outer(np.arange(128), np.arange(128)) / 128.0
    C1, S1 = np.cos(a1), np.sin(a1)
    c["c_m1"] = np.concatenate([C1, -S1, S1, C1], axis=1).astype(bf)
    atw = 2 * np.pi * np.outer(np.arange(N2), np.arange(128)) / float(S)
    tw = np.concatenate([np.cos(atw).reshape(-1), np.sin(atw).reshape(-1)])
    c["c_tw"] = np.ascontiguousarray(np.broadcast_to(tw[None, :], (128, tw.size))).astype(bf)
    a3 = 2 * np.pi * np.outer(np.arange(N2), np.arange(N2)) / float(N2)
    sc = 1.0 / math.sqrt(S)
    K3c = np.kron(np.eye(G), np.cos(a3)) * sc
    K3s = np.kron(np.eye(G), np.sin(a3)) * sc
    c["c_k3"] = np.concatenate([K3c, K3s], axis=1).astype(bf)
    c["c_onesb"] = np.ones((128, 128), bf)
    c["c_ident"] = np.eye(128).astype(bf)
    return c


def _tile_w(W, nk, no):
    return np.ascontiguousarray(W.reshape(nk, 128, no, 128).transpose(2, 1, 0, 3)).reshape(no * 128, nk * 128)


def _prep_weights(inp):
    L = inp["w_in"].shape[0]
    out = {}
    for ab, sfx in (("a", "ffa"), ("b", "ffb")):
        wgu = np.empty((L, FC * 128, 2 * KC * 128), np.float32)
        wd = np.empty((L, KC * 128, FC * 128), np.float32)
        for l in range(L):
            g = _tile_w(np.asarray(inp[sfx + "_gate"][l]), KC, FC).reshape(FC, 128, 1, KC * 128)
            u = _tile_w(np.asarray(inp[sfx + "_up"][l]), KC, FC).reshape(FC, 128, 1, KC * 128)
            wgu[l] = np.concatenate([g, u], axis=2).reshape(FC * 128, 2 * KC * 128)
            wd[l] = _tile_w(np.asarray(inp[sfx + "_down"][l]), FC, KC)
        out["wgu_" + ab] = wgu
        out["wd_" + ab] = wd
    out["win"] = np.stack([_tile_w(np.asarray(inp["w_in"][l]), KC, 64) for l in range(L)])
    out["pf"] = np.stack([_tile_w(np.asarray(inp["p_f"][l]), 8, 16) for l in range(L)])
    out["pa"] = np.stack([_tile_w(np.asarray(inp["p_a"][l]), 8, 16) for l in range(L)])
    out["wo"] = np.stack([_tile_w(np.asarray(inp["w_o"][l]), 16, 16) for l in range(L)])
    gains = np.empty((L, 128, 64), np.float32)
    small = np.zeros((L, 128, 4), np.float32)
    lamv = np.empty((L, 128, 256), np.float32)
    for l in range(L):
        for gi, nm in enumerate(("norm_ffa", "norm_mix", "norm_ffb", "norm_out")):
            gains[l, :, gi * 16:(gi + 1) * 16] = np.asarray(inp[nm][l]).reshape(16, 128).T
        small[l, :, 0] = np.tile(np.asarray(inp["q_norm"][l]), 2)
        small[l, :, 1] = np.tile(np.asarray(inp["k_norm"][l]), 2)
        small[l, :, 2] = np.asarray(inp["subln"][l])
        lv = np.concatenate([np.asarray(inp[k][l]) for k in ("lambda_q1", "lambda_k1", "lambda_q2", "lambda_k2")])
        lamv[l] = np.broadcast_to(lv[None, :], (128, 256))
    out["gains"] = gains
    out["small"] = small
    out["lamv"] = lamv
    return out


def kernel(**inputs):
    x = np.asarray(inputs["x"], dtype=np.float32)
    B, S, _ = x.shape
    L = inputs["w_in"].shape[0]
    nc = build_nc(S, L)
    common = _prep_weights(inputs)
    common.update(_consts(S))
    in_maps = [dict(common, xT=np.ascontiguousarray(x[b].T)) for b in range(B)]
    res = run_bass_kernel_spmd(nc, in_maps, core_ids=list(range(B)))
    if DBG:
        kernel.dbg = res.results
    out = np.stack([np.ascontiguousarray(np.asarray(res.results[b]["outT"]).T) for b in range(B)])
    return out.astype(np.float32)
```

```python
import math
from contextlib import ExitStack
import numpy as np
import ml_dtypes
import concourse.bass as bass
import concourse.mybir as mybir
from concourse.bass_utils import run_bass_kernel_spmd

F32 = mybir.dt.float32
BF16 = mybir.dt.bfloat16
ALU = mybir.AluOpType
AF = mybir.ActivationFunctionType

D = 2048
KC = 16
DFF = 5632
FC = 44
NH = 8
T = 512
EPS = 1e-6
ROPE_THETA = 500000.0


class Buf:
    __slots__ = ("w", "r")

    def __init__(self):
        self.w = None
        self.r = []


class BP:
    ENG = ("pe", "act", "dve", "pool", "sp")
    UID = 0
    NBLK = 0
    LVL = 9
    REV = False
    CNT = {}
    SEM = {}
    GST = None
    STOP = 10 ** 9

    def __init__(self, nc):
        self.nc = nc
        self.ops = {e: [] for e in self.ENG}
        self.cnt = BP.CNT
        self.sem = BP.SEM
        self.st = ExitStack()
        self.bufs = set()

    def _sem(self, name):
        if name not in self.sem:
            self.sem[name] = BP.GST.enter_context(self.nc.semaphore(name))
            self.cnt[name] = 0

    def op(self, eng, fn, reads=(), writes=(), dma=None, signal=True):
        deps = []
        for b in reads:
            if b.w is not None:
                deps.append(b.w)
        for b in writes:
            if b.w is not None:
                deps.append(b.w)
            deps.extend(b.r)
        if dma:
            semn, inc = "d_" + dma, 16
        else:
            semn, inc = "c_" + eng, 1
        self._sem(semn)
        if signal:
            self.cnt[semn] += inc
            ev = (semn, self.cnt[semn])
        else:
            ev = (semn, self.cnt[semn] + inc)
        if eng == "pe":
            deps = [d for d in deps if d[0] != "c_pe"]
        self.ops[eng].append((fn, deps, semn if signal else None, inc))
        for b in reads:
            b.r.append(ev)
            self.bufs.add(b)
        for b in writes:
            b.w = ev
            b.r = []
            self.bufs.add(b)
        return ev

    def run(self):
        nc = self.nc
        BP.NBLK += 1
        if BP.NBLK > BP.STOP:
            self.st.close()
            return
        final = dict(self.cnt)

        def mk(en):
            def body(e):
                waited = {}
                for fn, deps, semn, inc in self.ops[en]:
                    need = {}
                    for dn, dv in deps:
                        if dv > need.get(dn, 0):
                            need[dn] = dv
                    for dn, dv in need.items():
                        if waited.get(dn, 0) < dv:
                            e.wait_ge(self.sem[dn], dv)
                            waited[dn] = dv
                    ins = fn(e)
                    if semn is not None:
                        ins.then_inc(self.sem[semn], inc)
                if en == "sp":
                    for dn, dv in final.items():
                        if dv > 0 and waited.get(dn, 0) < dv:
                            e.wait_ge(self.sem[dn], dv)
            return body

        with nc.Block() as blk:
            blk.tensor(mk("pe"))
            blk.scalar(mk("act"))
            blk.vector(mk("dve"))
            blk.gpsimd(mk("pool"))
            blk.sync(mk("sp"))
        for b in self.bufs:
            b.w = None
            b.r = []
        self.st.close()


def lam_init_of(i):
    return 0.8 - 0.6 * math.exp(-0.3 * i)


DBG = False


def build_nc(S, DEPTH):
    N2 = S // 128
    G = 128 // N2
    NGRP = 128 // G
    NT = S // T
    NKT = S // 128
    nc = bass.Bass("TRN2", target_bir_lowering=False)
    BP.CNT = {}
    BP.SEM = {}
    BP.GST = ExitStack()
    BP.NBLK = 0

    def din(name, shape, dt=F32):
        return nc.dram_tensor(name, list(shape), dt, kind="ExternalInput").ap()

    def dscr(name, shape, dt=BF16):
        return nc.dram_tensor(name, list(shape), dt, kind="Internal").ap()

    xT = din("xT", [D, S])
    outT = nc.dram_tensor("outT", [D, S], F32, kind="ExternalOutput").ap()
    wgu_in = [din("wgu_a", [DEPTH, FC * 128, 2 * KC * 128]), din("wgu_b", [DEPTH, FC * 128, 2 * KC * 128])]
    wd_in = [din("wd_a", [DEPTH, KC * 128, FC * 128]), din("wd_b", [DEPTH, KC * 128, FC * 128])]
    win_in = din("win", [DEPTH, 64 * 128, KC * 128])
    pf_in = din("pf", [DEPTH, 16 * 128, 8 * 128])
    pa_in = din("pa", [DEPTH, 16 * 128, 8 * 128])
    wo_in = din("wo", [DEPTH, 16 * 128, 16 * 128])
    gains_in = din("gains", [DEPTH, 128, 4 * 16])
    small_in = din("small", [DEPTH, 128, 4])
    lamv_in = din("lamv", [DEPTH, 128, 256])
    c_ones = din("c_ones", [128, 128])
    c_blk = din("c_blk", [128, 128])
    c_rot = din("c_rot", [128, 128])
    c_cos = din("c_cos", [128, S])
    c_sin = din("c_sin", [128, S])
    c_cs = din("c_cs", [128, 2 * 512], BF16)
    c_m1 = din("c_m1", [128, 512], BF16)
    c_tw = din("c_tw", [128, 2 * N2 * 128], BF16)
    c_k3 = din("c_k3", [128, 256], BF16)
    c_onesb = din("c_onesb", [128, 128], BF16)
    c_ident = din("c_ident", [128, 128], BF16)

    xs = dscr("xs", [D, S], F32)
    wgu_bf = [dscr("wgu_bf0", [FC * 128, 2 * KC * 128]), dscr("wgu_bf1", [FC * 128, 2 * KC * 128])]
    wd_bf = [dscr("wd_bf0", [KC * 128, FC * 128]), dscr("wd_bf1", [KC * 128, FC * 128])]
    win_bf = dscr("win_bf", [64 * 128, KC * 128])
    pf_bf = dscr("pf_bf", [16 * 128, 8 * 128])
    pa_bf = dscr("pa_bf", [16 * 128, 8 * 128])
    wo_bf = dscr("wo_bf", [16 * 128, 16 * 128])
    qT_d = dscr("qT_d", [NH * 128, S])
    kT_d = dscr("kT_d", [NH * 128, S])
    v_d = dscr("v_d", [S, NH * 128])
    ab_d = dscr("ab_d", [S, 4 * 512])
    fT_d = dscr("fT_d", [8 * 128, S])
    oT_d = dscr("oT_d", [NH * 128, S])

    top = ExitStack()

    uid = [0]

    def sb(name, shape, dt=F32, st=top):
        uid[0] += 1
        return st.enter_context(nc.sbuf_tensor(f"s_{name}_{uid[0]}", list(shape), dt))

    PS = [top.enter_context(nc.psum_tensor(f"ps{i}", [128, 512], F32)) for i in range(7)]
    PSB = [Buf() for _ in range(7)]
    PTt = top.enter_context(nc.psum_tensor("pt", [128, 1024], BF16))
    PT = [PTt[:, 0:256], PTt[:, 256:512]]
    PTB = [Buf(), Buf()]

    ones_f = sb("ones_f", [128, 128]); blk_f = sb("blk_f", [128, 128]); rot_f = sb("rot_f", [128, 128])
    onesb = sb("onesb", [128, 128], BF16)
    identb = sb("identb", [128, 128], BF16)
    cs_b = sb("cs_b", [128, 2, 512], BF16)
    m1_b = sb("m1_b", [128, 2, 256], BF16)
    k3_b = sb("k3_b", [128, 2, 128], BF16)
    eps_t = sb("eps_t", [128, 1])
    gains = sb("gains", [128, 4, 16])
    small = sb("small", [128, 4])
    lamv = sb("lamv", [128, 4, 64])
    lamt = sb("lamt", [128, 8])
    CONST = Buf()
    LAYERC = Buf()

    bp = BP(nc)
    bp.op("sp", lambda e: e.dma_start(out=ones_f[:, :], in_=c_ones[:, :]), writes=[CONST], dma="c0")
    bp.op("sp", lambda e: e.dma_start(out=blk_f[:, :], in_=c_blk[:, :]), writes=[CONST], dma="c1")
    bp.op("sp", lambda e: e.dma_start(out=rot_f[:, :], in_=c_rot[:, :]), writes=[CONST], dma="c2")
    bp.op("sp", lambda e: e.dma_start(out=onesb[:, :], in_=c_onesb[:, :]), writes=[CONST], dma="c3")
    bp.op("sp", lambda e: e.dma_start(out=cs_b[:, :, :], in_=c_cs.rearrange("p (k c) -> p k c", k=2)), writes=[CONST], dma="c4")
    bp.op("sp", lambda e: e.dma_start(out=m1_b[:, :, :], in_=c_m1.rearrange("p (k c) -> p k c", k=2)), writes=[CONST], dma="c5")
    bp.op("sp", lambda e: e.dma_start(out=k3_b[:, :, :], in_=c_k3.rearrange("p (k c) -> p k c", k=2)), writes=[CONST], dma="c6")
    bp.op("sp", lambda e: e.dma_start(out=identb[:, :], in_=c_ident[:, :]), writes=[CONST], dma="c7")
    bp.op("dve", lambda e: e.memset(eps_t[:, :], EPS), writes=[CONST])
    for c in range(KC):
        bp.op("pool", lambda e, c=c: e.dma_start(out=xs[c * 128:(c + 1) * 128, :], in_=xT[c * 128:(c + 1) * 128, :]),
              writes=[Buf()], dma=f"x{c % 4}")
    bp.run()

    def cast_rows(bp, dst, src, nrows, tag):
        for i in range(nrows // 128):
            bp.op("pool", lambda e, i=i: e.dma_start(out=dst[i * 128:(i + 1) * 128, :], in_=src[i * 128:(i + 1) * 128, :]),
                  writes=[Buf()], dma=f"cast{i % 4}")

    def rmsnorm(bp, xt, XT, g_idx, tmp, dst_fn, DST, sqb, SQ):
        rt, rstd, RT, RSTD = tmp
        for c in range(KC):
            s = c % 4
            bp.op("act", lambda e, c=c, s=s: e.activation(out=sqb[:, s, :], in_=xt[:, c, :], func=AF.Square),
                  reads=[XT[c]], writes=[SQ[s]])
            bp.op("pe", lambda e, c=c, s=s: e.matmul(PS[6][:, :], lhsT=ones_f[:, :], rhs=sqb[:, s, :],
                                                    start=(c == 0), stop=(c == KC - 1)),
                  reads=[SQ[s], CONST], writes=[PSB[6]])
        bp.op("act", lambda e: e.activation(out=rt[:, :], in_=PS[6][:, :], func=AF.Sqrt, bias=eps_t[:, 0:1], scale=1.0 / D),
              reads=[PSB[6], CONST], writes=[RT])
        bp.op("dve", lambda e: e.reciprocal(out=rstd[:, :], in_=rt[:, :]), reads=[RT], writes=[RSTD])
        for c in range(KC):
            bp.op("dve", lambda e, c=c: e.scalar_tensor_tensor(out=dst_fn(c), in0=xt[:, c, :], scalar=gains[:, g_idx, c:c + 1],
                                                              in1=rstd[:, :], op0=ALU.mult, op1=ALU.mult),
                  reads=[XT[c], RSTD, LAYERC], writes=[DST[c]])

    def ffn(bp, xt, XT, ht, HT, at, AT, wgu, WGU, wd, WD, sg, SG, wgu_src, wd_src):
        for fc in range(FC):
            s = fc % 2
            bp.op("sp", lambda e, fc=fc, s=s: e.dma_start(
                out=wgu[s][:, :, :, :], in_=wgu_src[fc * 128:(fc + 1) * 128, :].rearrange("p (g k f) -> p g k f", g=2, k=KC)),
                writes=[WGU[s]], dma=f"wgu{s}")
            pg, pu = 2 * s, 2 * s + 1
            for kc in range(KC):
                bp.op("pe", lambda e, kc=kc, s=s, pg=pg: e.matmul(PS[pg][:, :], lhsT=wgu[s][:, 0, kc, :], rhs=ht[:, kc, :],
                                                                 start=(kc == 0), stop=(kc == KC - 1)),
                      reads=[WGU[s], HT[kc]], writes=[PSB[pg]], signal=(kc == KC - 1))
            for kc in range(KC):
                bp.op("pe", lambda e, kc=kc, s=s, pu=pu: e.matmul(PS[pu][:, :], lhsT=wgu[s][:, 1, kc, :], rhs=ht[:, kc, :],
                                                                 start=(kc == 0), stop=(kc == KC - 1)),
                      reads=[WGU[s], HT[kc]], writes=[PSB[pu]], signal=(kc == KC - 1))
            bp.op("act", lambda e, s=s, pg=pg: e.activation(out=sg[s][:, :], in_=PS[pg][:, :], func=AF.Silu),
                  reads=[PSB[pg]], writes=[SG[s]])
            bp.op("dve", lambda e, fc=fc, s=s, pu=pu: e.tensor_tensor(out=at[:, fc, :], in0=sg[s][:, :], in1=PS[pu][:, :], op=ALU.mult),
                  reads=[SG[s], PSB[pu]], writes=[AT[fc]])
        for dc in range(KC):
            s = dc % 2
            bp.op("sp", lambda e, dc=dc, s=s: e.dma_start(
                out=wd[s][:, :, :], in_=wd_src[dc * 128:(dc + 1) * 128, :].rearrange("p (k f) -> p k f", k=FC)),
                writes=[WD[s]], dma=f"wd{s}")
            py = 4 + s
            for fc in range(FC):
                bp.op("pe", lambda e, fc=fc, s=s, py=py: e.matmul(PS[py][:, :], lhsT=wd[s][:, fc, :], rhs=at[:, fc, :],
                                                                 start=(fc == 0), stop=(fc == FC - 1)),
                      reads=[WD[s], AT[fc]], writes=[PSB[py]], signal=(fc == FC - 1))
            bp.op("dve", lambda e, dc=dc, py=py: e.scalar_tensor_tensor(out=xt[:, dc, :], in0=PS[py][:, :], scalar=0.5,
                                                                       in1=xt[:, dc, :], op0=ALU.mult, op1=ALU.add),
                  reads=[PSB[py], XT[dc]], writes=[XT[dc]])

    def proj(bp, ws, WS, slot, src_rows, nk, rhs_fn, RHS, pbank, tag):
        bp.op("sp", lambda e: e.dma_start(out=ws[slot][:, 0:nk, :], in_=src_rows.rearrange("p (k f) -> p k f", k=nk)),
              writes=[WS[slot]], dma=f"{tag}{slot}")
        for kc in range(nk):
            bp.op("pe", lambda e, kc=kc: e.matmul(PS[pbank][:, :], lhsT=ws[slot][:, kc, :], rhs=rhs_fn(kc),
                                                  start=(kc == 0), stop=(kc == nk - 1)),
                  reads=[WS[slot], RHS[kc]], writes=[PSB[pbank]], signal=(kc == nk - 1))

    for l in range(DEPTH):
        li = lam_init_of(l)
        bp = BP(nc)
        for ab in range(2):
            cast_rows(bp, wgu_bf[ab], wgu_in[ab][l], FC * 128, f"g{ab}")
            cast_rows(bp, wd_bf[ab], wd_in[ab][l], KC * 128, f"d{ab}")
        cast_rows(bp, win_bf, win_in[l], 64 * 128, "wi")
        cast_rows(bp, pf_bf, pf_in[l], 16 * 128, "pf")
        cast_rows(bp, pa_bf, pa_in[l], 16 * 128, "pa")
        cast_rows(bp, wo_bf, wo_in[l], 16 * 128, "wo")
        bp.op("sp", lambda e, l=l: e.dma_start(out=gains[:, :, :], in_=gains_in[l].rearrange("p (g c) -> p g c", g=4)),
              writes=[LAYERC], dma="l0")
        bp.op("sp", lambda e, l=l: e.dma_start(out=small[:, :], in_=small_in[l]), writes=[LAYERC], dma="l1")
        LV = Buf()
        bp.op("sp", lambda e, l=l: e.dma_start(out=lamv[:, :, :], in_=lamv_in[l].rearrange("p (g c) -> p g c", g=4)),
              writes=[LV], dma="l2")
        LT = Buf()
        bp.op("dve", lambda e: e.tensor_tensor(out=lamv[:, 0, :], in0=lamv[:, 0, :], in1=lamv[:, 1, :], op=ALU.mult), reads=[LV], writes=[LV])
        bp.op("dve", lambda e: e.tensor_tensor(out=lamv[:, 2, :], in0=lamv[:, 2, :], in1=lamv[:, 3, :], op=ALU.mult), reads=[LV], writes=[LV])
        bp.op("dve", lambda e: e.tensor_reduce(out=lamt[:, 0:1], in_=lamv[:, 0, :], axis=mybir.AxisListType.X, op=ALU.add), reads=[LV], writes=[LT])
        bp.op("dve", lambda e: e.tensor_reduce(out=lamt[:, 1:2], in_=lamv[:, 2, :], axis=mybir.AxisListType.X, op=ALU.add), reads=[LT, LV], writes=[LT])
        bp.op("act", lambda e: e.activation(out=lamt[:, 2:4], in_=lamt[:, 0:2], func=AF.Exp), reads=[LT], writes=[LT])
        bp.op("dve", lambda e, li=li: e.scalar_tensor_tensor(out=lamt[:, 4:5], in0=lamt[:, 2:3], scalar=li, in1=lamt[:, 3:4],
                                                             op0=ALU.add, op1=ALU.subtract), reads=[LT], writes=[LT])
        bp.op("dve", lambda e: e.tensor_scalar(out=lamt[:, 5:6], in0=lamt[:, 4:5], scalar1=-1.0, scalar2=None, op0=ALU.mult),
              reads=[LT], writes=[LT])
        bp.op("dve", lambda e, li=li: e.tensor_scalar(out=small[:, 3:4], in0=small[:, 2:3], scalar1=(1.0 - li), scalar2=None, op0=ALU.mult),
              reads=[LAYERC], writes=[LAYERC])
        bp.run()

        for phase in ("A", "C"):
            st = ExitStack()
            xt = sb("xt", [128, KC, T], F32, st)
            ht = sb("ht", [128, KC, T], BF16, st)
            big = sb("big", [128, FC * T], BF16, st)
            at = big[:, :].rearrange("p (k t) -> p k t", k=FC)
            wgu = [sb(f"wgu{i}", [128, 2, KC, 128], BF16, st) for i in range(2)]
            wd = [sb(f"wd{i}", [128, FC, 128], BF16, st) for i in range(2)]
            ws = [sb(f"ws{i}", [128, KC, 128], BF16, st) for i in range(2)]
            sqb = sb("sqb", [128, 4, T], F32, st)
            sg = [sb(f"sg{i}", [128, T], F32, st) for i in range(2)]
            tmp = [sb(f"tmp{i}", [128, T], F32, st) for i in range(8)]
            qo = [sb(f"qo{i}", [128, T], BF16, st) for i in range(2)]
            cst = sb("cst", [128, 2, T], F32, st)
            XT = [Buf() for _ in range(KC)]; HT = [Buf() for _ in range(KC)]; AT = [Buf() for _ in range(FC)]
            WGU = [Buf(), Buf()]; WD = [Buf(), Buf()]; WS = [Buf(), Buf()]; SQ = [Buf() for _ in range(4)]
            SG = [Buf(), Buf()]; TMP = [Buf() for _ in range(8)]; QO = [Buf(), Buf()]; CST = Buf()
            ut = big[:, 0:8 * T].rearrange("p (k t) -> p k t", k=8)
            vo = big[:, 8 * T:16 * T].rearrange("p (s c) -> p s c", s=4)
            abo = big[:, 16 * T:32 * T].rearrange("p (s g c) -> p s g c", s=4, g=4)
            UT = [Buf() for _ in range(8)]; VO = Buf(); ABO = Buf()
            ftt = big[:, 0:8 * T].rearrange("p (k t) -> p k t", k=8)
            ott = big[:, 8 * T:16 * T].rearrange("p (k t) -> p k t", k=8)
            mt = big[:, 16 * T:32 * T].rearrange("p (k t) -> p k t", k=16)
            FTT = [Buf() for _ in range(8)]; OTT = [Buf() for _ in range(8)]; MT = [Buf() for _ in range(16)]

            for ti in (range(NT) if (phase == "A" or not BP.REV) else reversed(range(NT))):
                t0 = ti * T
                bp = BP(nc)
                bp.op("sp", lambda e: e.dma_start(out=xt[:, :, :], in_=xs[:, t0:t0 + T].rearrange("(c p) t -> p c t", p=128)),
                      writes=XT, dma="xt")
                norm_tmp = (tmp[0], tmp[1], TMP[0], TMP[1])
                if phase == "A":
                    rmsnorm(bp, xt, XT, 0, norm_tmp, lambda c: ht[:, c, :], HT, sqb, SQ)
                    ffn(bp, xt, XT, ht, HT, at, AT, wgu, WGU, wd, WD, sg, SG, wgu_bf[0], wd_bf[0])
                    bp.op("pool", lambda e: e.dma_start(out=xs[:, t0:t0 + T].rearrange("(c p) t -> p c t", p=128), in_=xt[:, :, :]),
                          reads=XT, dma="xo")
                    rmsnorm(bp, xt, XT, 1, norm_tmp, lambda c: ht[:, c, :], HT, sqb, SQ)
                    bp.op("sp", lambda e: e.dma_start(out=cst[:, 0, :], in_=c_cos[:, t0:t0 + T]), writes=[CST], dma="cs0")
                    bp.op("sp", lambda e: e.dma_start(out=cst[:, 1, :], in_=c_sin[:, t0:t0 + T]), writes=[CST], dma="cs1")
                    n = 0
                    for oc in range(8):
                        s = n % 2; pb = n % 2; n += 1
                        proj(bp, ws, WS, s, win_bf[oc * 128:(oc + 1) * 128, :], KC, lambda kc: ht[:, kc, :], HT, pb, "ws")
                        bp.op("act", lambda e, oc=oc, pb=pb: e.activation(out=ut[:, oc, :], in_=PS[pb][:, :], func=AF.Copy),
                              reads=[PSB[pb]], writes=[UT[oc]])
                    for oc in range(8, 24):
                        s = n % 2; pb = n % 2; n += 1
                        isq = oc < 16
                        hd = oc - 8 if isq else oc - 16
                        dstd = qT_d if isq else kT_d
                        gcol = 0 if isq else 1
                        proj(bp, ws, WS, s, win_bf[oc * 128:(oc + 1) * 128, :], KC, lambda kc: ht[:, kc, :], HT, pb, "ws")
                        bp.op("act", lambda e, pb=pb: e.activation(out=tmp[2][:, :], in_=PS[pb][:, :], func=AF.Square),
                              reads=[PSB[pb]], writes=[TMP[2]])
                        bp.op("pe", lambda e: e.matmul(PS[2][:, :], lhsT=blk_f[:, :], rhs=tmp[2][:, :], start=True, stop=True),
                              reads=[TMP[2], CONST], writes=[PSB[2]])
                        bp.op("act", lambda e: e.activation(out=tmp[3][:, :], in_=PS[2][:, :], func=AF.Sqrt, bias=eps_t[:, 0:1], scale=1.0 / 64),
                              reads=[PSB[2]], writes=[TMP[3]])
                        bp.op("dve", lambda e: e.reciprocal(out=tmp[4][:, :], in_=tmp[3][:, :]), reads=[TMP[3]], writes=[TMP[4]])
                        bp.op("dve", lambda e, pb=pb, gcol=gcol: e.scalar_tensor_tensor(
                            out=tmp[5][:, :], in0=PS[pb][:, :], scalar=small[:, gcol:gcol + 1], in1=tmp[4][:, :], op0=ALU.mult, op1=ALU.mult),
                            reads=[PSB[pb], TMP[4], LAYERC], writes=[TMP[5]])
                        bp.op("pe", lambda e: e.matmul(PS[3][:, :], lhsT=rot_f[:, :], rhs=tmp[5][:, :], start=True, stop=True),
                              reads=[TMP[5], CONST], writes=[PSB[3]])
                        bp.op("dve", lambda e: e.tensor_tensor(out=tmp[6][:, :], in0=tmp[5][:, :], in1=cst[:, 0, :], op=ALU.mult),
                              reads=[TMP[5], CST], writes=[TMP[6]])
                        bp.op("dve", lambda e: e.tensor_tensor(out=tmp[7][:, :], in0=PS[3][:, :], in1=cst[:, 1, :], op=ALU.mult),
                              reads=[PSB[3], CST], writes=[TMP[7]])
                        q = hd % 2
                        bp.op("pool", lambda e, q=q: e.tensor_tensor(out=qo[q][:, :], in0=tmp[6][:, :], in1=tmp[7][:, :], op=ALU.add),
                              reads=[TMP[6], TMP[7]], writes=[QO[q]])
                        bp.op("pool", lambda e, q=q, hd=hd, dstd=dstd: e.dma_start(out=dstd[hd * 128:(hd + 1) * 128, t0:t0 + T], in_=qo[q][:, :]),
                              reads=[QO[q]], dma=f"qo{q}")
                    for oc in range(24, 32):
                        s = n % 2; n += 1
                        bp.op("sp", lambda e, oc=oc, s=s: e.dma_start(
                            out=ws[s][:, :, :], in_=win_bf[oc * 128:(oc + 1) * 128, :].rearrange("p (k f) -> p k f", k=KC)),
                            writes=[WS[s]], dma=f"ws{s}")
                        for sub in range(4):
                            pb = 4 + (sub % 2)
                            for kc in range(KC):
                                bp.op("pe", lambda e, kc=kc, sub=sub, s=s, pb=pb: e.matmul(
                                    PS[pb][:, 0:128], lhsT=ht[:, kc, sub * 128:(sub + 1) * 128], rhs=ws[s][:, kc, :],
                                    start=(kc == 0), stop=(kc == KC - 1)),
                                    reads=[WS[s], HT[kc]], writes=[PSB[pb]], signal=(kc == KC - 1))
                            bp.op("act", lambda e, sub=sub, oc=oc, pb=pb: e.activation(
                                out=vo[:, sub, (oc - 24) * 128:(oc - 23) * 128], in_=PS[pb][:, 0:128], func=AF.Copy),
                                reads=[PSB[pb]], writes=[VO])
                    bp.op("pool", lambda e: e.dma_start(out=v_d[t0:t0 + T, :].rearrange("(s p) c -> p s c", p=128), in_=vo),
                          reads=[VO], dma="vo")
                    for sub in range(4):
                        for g in range(4):
                            pb = 4 + (g % 2)
                            for kc in range(2):
                                bp.op("pe", lambda e, kc=kc, sub=sub, g=g, pb=pb: e.matmul(
                                    PS[pb][:, :], lhsT=ut[:, 2 * g + kc, sub * 128:(sub + 1) * 128], rhs=cs_b[:, kc, :],
                                    start=(kc == 0), stop=(kc == 1)),
                                    reads=[UT[2 * g + kc], CONST], writes=[PSB[pb]], signal=(kc == 1))
                            bp.op("dve", lambda e, sub=sub, g=g, pb=pb: e.tensor_copy(out=abo[:, sub, g, :], in_=PS[pb][:, :]),
                                  reads=[PSB[pb]], writes=[ABO])
                    bp.op("pool", lambda e: e.dma_start(out=ab_d[t0:t0 + T, :].rearrange("(s p) c -> p s c", p=128),
                                                        in_=abo.rearrange("p s g c -> p s (g c)")),
                          reads=[ABO], dma="abo")
                else:
                    rmsnorm(bp, xt, XT, 1, norm_tmp, lambda c: ht[:, c, :], HT, sqb, SQ)
                    bp.op("sp", lambda e: e.dma_start(out=ftt, in_=fT_d[:, t0:t0 + T].rearrange("(c p) t -> p c t", p=128)),
                          writes=FTT, dma="ft")
                    bp.op("sp", lambda e: e.dma_start(out=ott, in_=oT_d[:, t0:t0 + T].rearrange("(c p) t -> p c t", p=128)),
                          writes=OTT, dma="ot")
                    for dc in range(16):
                        proj(bp, ws, WS, 0, win_bf[(32 + dc) * 128:(33 + dc) * 128, :], KC, lambda kc: ht[:, kc, :], HT, 0, "ws")
                        bp.op("act", lambda e: e.activation(out=tmp[2][:, :], in_=PS[0][:, :], func=AF.Sigmoid), reads=[PSB[0]], writes=[TMP[2]])
                        proj(bp, ws, WS, 1, win_bf[(48 + dc) * 128:(49 + dc) * 128, :], KC, lambda kc: ht[:, kc, :], HT, 1, "ws")
                        bp.op("act", lambda e: e.activation(out=tmp[3][:, :], in_=PS[1][:, :], func=AF.Sigmoid), reads=[PSB[1]], writes=[TMP[3]])
                        proj(bp, ws, WS, 0, pf_bf[dc * 128:(dc + 1) * 128, :], 8, lambda kc: ftt[:, kc, :], FTT, 2, "ws")
                        proj(bp, ws, WS, 1, pa_bf[dc * 128:(dc + 1) * 128, :], 8, lambda kc: ott[:, kc, :], OTT, 3, "ws")
                        bp.op("dve", lambda e: e.tensor_tensor(out=tmp[4][:, :], in0=tmp[2][:, :], in1=PS[2][:, :], op=ALU.mult),
                              reads=[TMP[2], PSB[2]], writes=[TMP[4]])
                        bp.op("dve", lambda e: e.tensor_tensor(out=tmp[5][:, :], in0=tmp[3][:, :], in1=PS[3][:, :], op=ALU.mult),
                              reads=[TMP[3], PSB[3]], writes=[TMP[5]])
                        bp.op("pool", lambda e, dc=dc: e.tensor_tensor(out=mt[:, dc, :], in0=tmp[4][:, :], in1=tmp[5][:, :], op=ALU.add),
                              reads=[TMP[4], TMP[5]], writes=[MT[dc]])
                    for dc in range(16):
                        s = dc % 2
                        proj(bp, ws, WS, s, wo_bf[dc * 128:(dc + 1) * 128, :], KC, lambda kc: mt[:, kc, :], MT, 4 + s, "ws")
                        bp.op("dve", lambda e, dc=dc, s=s: e.tensor_tensor(out=xt[:, dc, :], in0=xt[:, dc, :], in1=PS[4 + s][:, :], op=ALU.add),
                              reads=[PSB[4 + s], XT[dc]], writes=[XT[dc]])
                    rmsnorm(bp, xt, XT, 2, norm_tmp, lambda c: ht[:, c, :], HT, sqb, SQ)
                    ffn(bp, xt, XT, ht, HT, at, AT, wgu, WGU, wd, WD, sg, SG, wgu_bf[1], wd_bf[1])
                    rmsnorm(bp, xt, XT, 3, norm_tmp, lambda c: xt[:, c, :], XT, sqb, SQ)
                    dst = outT if l == DEPTH - 1 else xs
                    bp.op("pool", lambda e, dst=dst: e.dma_start(out=dst[:, t0:t0 + T].rearrange("(c p) t -> p c t", p=128), in_=xt[:, :, :]),
                          reads=XT, dma="xo")
                bp.run()
            st.close()

            if phase == "C":
                continue
            st = ExitStack()
            NPR = 8
            LOOK = 5
            SBK = [0, 1, 2, 5, 6]
            acc = [sb(f"acc{i}", [128, T], F32, st) for i in range(2)]
            ACC = [Buf(), Buf()]
            qs_ = sb("qs_", [128, S], BF16, st); ks_ = sb("ks_", [128, S], BF16, st)
            vs_ = sb("vs_", [128, NKT, 128], BF16, st)
            pr = [sb(f"pr{i}", [128, T], BF16, st) for i in range(NPR)]
            fo = [sb(f"fo{i}", [128, T], F32, st) for i in range(4)]
            ft = [sb(f"ft{i}", [128, T], F32, st) for i in range(4)]
            oo = [sb(f"oo{i}", [128, T], BF16, st) for i in range(2)]
            QS = Buf(); KS = Buf(); VS = Buf(); PR = [Buf() for _ in range(NPR)]
            FO = [Buf() for _ in range(4)]; FTB = [Buf() for _ in range(4)]; OO = [Buf(), Buf()]
            for h in range(NH):
                bp = BP(nc)
                bp.op("sp", lambda e, h=h: e.dma_start(out=qs_[:, :], in_=qT_d[h * 128:(h + 1) * 128, :]), writes=[QS], dma="q")
                bp.op("sp", lambda e, h=h: e.dma_start(out=ks_[:, :], in_=kT_d[h * 128:(h + 1) * 128, :]), writes=[KS], dma="k")
                bp.op("sp", lambda e, h=h: e.dma_start(out=vs_[:, :, :], in_=v_d[:, h * 128:(h + 1) * 128].rearrange("(k p) c -> p k c", p=128)),
                      writes=[VS], dma="v")
                units = [(qc, kt, cp) for qc in range(NT) for kt in range(NKT) for cp in range(2)]
                U = len(units)
                ctr = [0]

                def emit_qk(u):
                    qc, kt, cp = units[u]
                    q0 = qc * T
                    pb = SBK[ctr[0] % 5]
                    ctr[0] += 1
                    r = u % NPR
                    bp.op("pe", lambda e, kt=kt, cp=cp, pb=pb, q0=q0: e.matmul(
                        PS[pb][:, :], lhsT=ks_[cp * 64:(cp + 1) * 64, kt * 128:(kt + 1) * 128],
                        rhs=qs_[cp * 64:(cp + 1) * 64, q0:q0 + T], start=True, stop=True),
                        reads=[KS, QS], writes=[PSB[pb]])
                    bp.op("act", lambda e, pb=pb, r=r: e.activation(out=pr[r][:, :], in_=PS[pb][:, :], func=AF.Exp, scale=0.125),
                          reads=[PSB[pb]], writes=[PR[r]])

                def fin1(qc):
                    bp.op("act", lambda e: e.activation(out=fo[0][:, :], in_=PS[3][:, :], func=AF.Copy), reads=[PSB[3]], writes=[FO[0]])
                    bp.op("dve", lambda e: e.tensor_copy(out=fo[1][:, :], in_=PS[4][:, :]), reads=[PSB[4]], writes=[FO[1]])
                    b0 = SBK[ctr[0] % 5]
                    b1 = SBK[(ctr[0] + 1) % 5]
                    ctr[0] += 2
                    bp.op("pe", lambda e, b0=b0: e.matmul(PS[b0][:, :], lhsT=ones_f[:, :], rhs=acc[0][:, :], start=True, stop=True),
                          reads=[ACC[0], CONST], writes=[PSB[b0]])
                    bp.op("pe", lambda e, b1=b1: e.matmul(PS[b1][:, :], lhsT=ones_f[:, :], rhs=acc[1][:, :], start=True, stop=True),
                          reads=[ACC[1], CONST], writes=[PSB[b1]])
                    bp.op("dve", lambda e, b0=b0: e.reciprocal(out=ft[0][:, :], in_=PS[b0][:, :]), reads=[PSB[b0]], writes=[FTB[0]])
                    bp.op("dve", lambda e, b1=b1: e.reciprocal(out=ft[1][:, :], in_=PS[b1][:, :]), reads=[PSB[b1]], writes=[FTB[1]])
                    bp.op("pool", lambda e: e.tensor_tensor(out=ft[0][:, :], in0=ft[0][:, :], in1=fo[0][:, :], op=ALU.mult),
                          reads=[FTB[0], FO[0]], writes=[FTB[0]])
                    bp.op("dve", lambda e: e.tensor_tensor(out=ft[1][:, :], in0=ft[1][:, :], in1=fo[1][:, :], op=ALU.mult),
                          reads=[FTB[1], FO[1]], writes=[FTB[1]])
                    bp.op("dve", lambda e: e.scalar_tensor_tensor(out=ft[2][:, :], in0=ft[1][:, :], scalar=lamt[:, 5:6], in1=ft[0][:, :],
                                                                  op0=ALU.mult, op1=ALU.add),
                          reads=[FTB[1], FTB[0]], writes=[FTB[2]])
                    bp.op("act", lambda e: e.activation(out=ft[3][:, :], in_=ft[2][:, :], func=AF.Square), reads=[FTB[2]], writes=[FTB[3]])

                def fin2(qc):
                    q0 = qc * T
                    pb = SBK[ctr[0] % 5]
                    ctr[0] += 1
                    bp.op("pe", lambda e, pb=pb: e.matmul(PS[pb][:, :], lhsT=ones_f[:, :], rhs=ft[3][:, :], start=True, stop=True),
                          reads=[FTB[3], CONST], writes=[PSB[pb]])
                    bp.op("act", lambda e, pb=pb: e.activation(out=ft[0][:, :], in_=PS[pb][:, :], func=AF.Sqrt, bias=eps_t[:, 0:1], scale=1.0 / 128),
                          reads=[PSB[pb]], writes=[FTB[0]])
                    bp.op("dve", lambda e: e.reciprocal(out=ft[1][:, :], in_=ft[0][:, :]), reads=[FTB[0]], writes=[FTB[1]])
                    o = qc % 2
                    bp.op("dve", lambda e, o=o: e.scalar_tensor_tensor(out=oo[o][:, :], in0=ft[2][:, :], scalar=small[:, 3:4], in1=ft[1][:, :],
                                                                       op0=ALU.mult, op1=ALU.mult),
                          reads=[FTB[2], FTB[1]], writes=[OO[o]])
                    bp.op("pool", lambda e, o=o, h=h, q0=q0: e.dma_start(out=oT_d[h * 128:(h + 1) * 128, q0:q0 + T], in_=oo[o][:, :]),
                          reads=[OO[o]], dma=f"oo{o}")

                for u in range(min(LOOK, U)):
                    emit_qk(u)
                pending = None
                since = 0
                for u in range(U):
                    qc, kt, cp = units[u]
                    r = u % NPR
                    bp.op("pe", lambda e, kt=kt, cp=cp, r=r: e.matmul(
                        PS[3 + cp][:, :], lhsT=vs_[:, kt, :], rhs=pr[r][:, :], start=(kt == 0), stop=(kt == NKT - 1)),
                        reads=[VS, PR[r]], writes=[PSB[3 + cp]])
                    if kt == 0:
                        bp.op("dve", lambda e, cp=cp, r=r: e.tensor_copy(out=acc[cp][:, :], in_=pr[r][:, :]),
                              reads=[PR[r]], writes=[ACC[cp]])
                    else:
                        bp.op("dve", lambda e, cp=cp, r=r: e.tensor_tensor(out=acc[cp][:, :], in0=acc[cp][:, :], in1=pr[r][:, :], op=ALU.add),
                              reads=[PR[r], ACC[cp]], writes=[ACC[cp]])
                    if u + LOOK < U:
                        emit_qk(u + LOOK)
                    since += 1
                    if pending is not None and since >= 12:
                        fin2(pending)
                        pending = None
                    if kt == NKT - 1 and cp == 1:
                        if pending is not None:
                            fin2(pending)
                        fin1(qc)
                        pending = qc
                        since = 0
                if pending is not None:
                    fin2(pending)
                bp.run()
            st.close()

            st = ExitStack()
            xsb = sb("xsb", [128, N2, 256], BF16, st)
            tw = sb("tw", [128, 2, N2, 128], BF16, st)
            yp = sb("yp", [128, 256, N2], BF16, st)
            fts = sb("fts", [128, S], BF16, st)
            fq = [sb(f"fq{i}", [128, 256], F32, st) for i in range(4)]
            yt = [sb(f"yt{i}", [128, 2, 128], BF16, st) for i in range(2)]
            XSB = Buf(); TW = Buf(); YP = [Buf() for _ in range(N2)]; FTS = Buf(); FQ = [Buf() for _ in range(4)]; YT = [Buf(), Buf()]
            for j in range(8):
                g, hf = j // 2, j % 2
                bp = BP(nc)
                bp.op("sp", lambda e: e.dma_start(out=tw[:, :, :, :], in_=c_tw.rearrange("p (a s t) -> p a s t", a=2, s=N2)), writes=[TW], dma="tw")
                bp.op("sp", lambda e, g=g, hf=hf: e.dma_start(
                    out=xsb[:, :, :], in_=ab_d[:, g * 512 + hf * 256: g * 512 + hf * 256 + 256].rearrange("(p s) c -> p s c", s=N2)),
                    writes=[XSB], dma="xsb")
                for s2 in range(N2 if BP.LVL >= 2 else 0):
                    pb = s2 % 2
                    bp.op("pe", lambda e, s2=s2, pb=pb: e.matmul(PS[pb][:, 0:256], lhsT=xsb[:, s2, 0:128], rhs=m1_b[:, 0, :], start=True, stop=False),
                          reads=[XSB, CONST], writes=[PSB[pb]], signal=False)
                    bp.op("pe", lambda e, s2=s2, pb=pb: e.matmul(PS[pb][:, 0:256], lhsT=xsb[:, s2, 128:256], rhs=m1_b[:, 1, :], start=False, stop=True),
                          reads=[XSB, CONST], writes=[PSB[pb]])
                    a, b = 2 * (s2 % 2), 2 * (s2 % 2) + 1
                    bp.op("dve", lambda e, s2=s2, pb=pb, a=a: e.tensor_tensor(out=fq[a][:, 0:128], in0=PS[pb][:, 0:128], in1=tw[:, 0, s2, :], op=ALU.mult),
                          reads=[PSB[pb], TW], writes=[FQ[a]])
                    bp.op("dve", lambda e, s2=s2, pb=pb, a=a: e.tensor_tensor(out=fq[a][:, 128:256], in0=PS[pb][:, 128:256], in1=tw[:, 0, s2, :], op=ALU.mult),
                          reads=[PSB[pb], TW, FQ[a]], writes=[FQ[a]])
                    bp.op("dve", lambda e, s2=s2, pb=pb, b=b: e.tensor_tensor(out=fq[b][:, 0:128], in0=PS[pb][:, 128:256], in1=tw[:, 1, s2, :], op=ALU.mult),
                          reads=[PSB[pb], TW], writes=[FQ[b]])
                    bp.op("dve", lambda e, s2=s2, pb=pb, b=b: e.tensor_tensor(out=fq[b][:, 128:256], in0=PS[pb][:, 0:128], in1=tw[:, 1, s2, :], op=ALU.mult),
                          reads=[PSB[pb], TW, FQ[b]], writes=[FQ[b]])
                    bp.op("pool", lambda e, s2=s2, a=a, b=b: e.tensor_tensor(out=yp[:, 0:128, s2], in0=fq[a][:, 0:128], in1=fq[b][:, 0:128], op=ALU.add),
                          reads=[FQ[a], FQ[b]], writes=[YP[s2]])
                    bp.op("pool", lambda e, s2=s2, a=a, b=b: e.tensor_tensor(out=yp[:, 128:256, s2], in0=fq[a][:, 128:256], in1=fq[b][:, 128:256], op=ALU.subtract),
                          reads=[FQ[a], FQ[b], YP[s2]], writes=[YP[s2]])
                for gp in range(NGRP if BP.LVL >= 3 else 0):
                    y = gp % 2
                    for ri in range(2):
                        src = yp[:, ri * 128 + gp * G: ri * 128 + gp * G + G, :].rearrange("p t s -> p (t s)")
                        bp.op("pe", lambda e, src=src, ri=ri, y=y: e.transpose(out=PT[y][:, ri * 128:(ri + 1) * 128], in_=src, identity=identb[:, :]),
                              reads=YP + [CONST], writes=[PTB[y]])
                    if BP.LVL < 4:
                        continue
                    bp.op("act", lambda e, y=y: e.activation(out=yt[y][:, :, :].rearrange("p a c -> p (a c)"), in_=PT[y], func=AF.Copy),
                          reads=[PTB[y]], writes=[YT[y]])
                    pb = 4 + y
                    bp.op("pe", lambda e, y=y, pb=pb: e.matmul(PS[pb][:, 0:128], lhsT=yt[y][:, 0, :], rhs=k3_b[:, 0, :], start=True, stop=False),
                          reads=[YT[y], CONST], writes=[PSB[pb]], signal=False)
                    bp.op("pe", lambda e, y=y, pb=pb: e.matmul(PS[pb][:, 0:128], lhsT=yt[y][:, 1, :], rhs=k3_b[:, 1, :], start=False, stop=True),
                          reads=[YT[y], CONST], writes=[PSB[pb]])
                    dstv = fts[:, :].rearrange("p (t2 t1) -> p t1 t2", t1=128)[:, gp * G:(gp + 1) * G, :]
                    bp.op("dve", lambda e, pb=pb, dstv=dstv: e.tensor_copy(out=dstv, in_=PS[pb][:, 0:128].rearrange("p (a b) -> p a b", a=G)),
                          reads=[PSB[pb]], writes=[FTS])
                bp.op("pool", lambda e, j=j: e.dma_start(out=fT_d[j * 128:(j + 1) * 128, :], in_=fts[:, :]), reads=[FTS], dma="fts")
                bp.run()
            st.close()
            if DBG and l == 0:
                bp = BP(nc)
                for nm, src, dt in (("q", qT_d, BF16), ("k", kT_d, BF16), ("o", oT_d, BF16), ("f", fT_d, BF16)):
                    dd = nc.dram_tensor("dbg_" + nm, [1024, S], dt, kind="ExternalOutput").ap()
                    bp.op("pool", lambda e, dd=dd, src=src: e.dma_start(out=dd[:, :], in_=src[:, :]), writes=[Buf()], dma="dbg" + nm)
                dd = nc.dram_tensor("dbg_x1", [D, S], F32, kind="ExternalOutput").ap()
                bp.op("pool", lambda e, dd=dd: e.dma_start(out=dd[:, :], in_=xs[:, :]), writes=[Buf()], dma="dbgx")
                dd = nc.dram_tensor("dbg_v", [S, 1024], BF16, kind="ExternalOutput").ap()
                bp.op("pool", lambda e, dd=dd: e.dma_start(out=dd[:, :], in_=v_d[:, :]), writes=[Buf()], dma="dbgv")
                dd = nc.dram_tensor("dbg_ab", [S, 2048], BF16, kind="ExternalOutput").ap()
                bp.op("pool", lambda e, dd=dd: e.dma_start(out=dd[:, :], in_=ab_d[:, :]), writes=[Buf()], dma="dbgab")
                bp.run()
    top.close()
    BP.GST.close()
    return nc


def _consts(S):
    N2 = S // 128
    G = 128 // N2
    bf = ml_dtypes.bfloat16
    c = {}
    c["c_ones"] = np.ones((128, 128), np.float32)
    blk = np.zeros((128, 128), np.float32)
    blk[:64, :64] = 1.0
    blk[64:, 64:] = 1.0
    c["c_blk"] = blk
    rot = np.zeros((128, 128), np.float32)
    for base in (0, 64):
        for i in range(8):
            rot[base + i + 8, base + i] = -1.0
            rot[base + i, base + i + 8] = 1.0
    c["c_rot"] = rot
    pos = np.arange(S, dtype=np.float32)
    inv = (np.float32(ROPE_THETA) ** (-(np.arange(0, 16, 2, dtype=np.float32)) / np.float32(16))).astype(np.float32)
    ang = (pos[None, :] * inv[:, None]).astype(np.float32)
    cosT = np.ones((128, S), np.float32)
    sinT = np.zeros((128, S), np.float32)
    for p in range(128):
        d = p % 64
        if d < 16:
            cosT[p] = np.cos(ang[d % 8])
            sinT[p] = np.sin(ang[d % 8])
    c["c_cos"] = cosT
    c["c_sin"] = sinT
    cc = np.arange(256)[:, None].astype(np.float64)
    cp = np.arange(256)[None, :].astype(np.float64)
    angc = 2 * np.pi * cc * cp / 256.0
    Cc = np.cos(angc) / 16.0
    Sc = -np.sin(angc) / 16.0
    cs = np.zeros((256, 512))
    for hf in range(2):
        cs[:, hf * 256:hf * 256 + 128] = Cc[:, hf * 128:(hf + 1) * 128]
        cs[:, hf * 256 + 128:hf * 256 + 256] = Sc[:, hf * 128:(hf + 1) * 128]
    c["c_cs"] = np.ascontiguousarray(cs.reshape(2, 128, 512).transpose(1, 0, 2).reshape(128, 1024)).astype(bf)
    a1 = 2 * np.pi * np.outer(np.arange(128), np.arange(128)) / 128.0
    C1, S1 = np.cos(a1), np.sin(a1)
    c["c_m1"] = np.concatenate([C1, -S1, S1, C1], axis=1).astype(bf)
    atw = 2 * np.pi * np.outer(np.arange(N2), np.arange(128)) / float(S)
    tw = np.concatenate([np.cos(atw).reshape(-1), np.sin(atw).reshape(-1)])
    c["c_tw"] = np.ascontiguousarray(np.broadcast_to(tw[None, :], (128, tw.size))).astype(bf)
    a3 = 2 * np.pi * np.outer(np.arange(N2), np.arange(N2)) / float(N2)
    sc = 1.0 / math.sqrt(S)
    K3c = np.kron(np.eye(G), np.cos(a3)) * sc
    K3s = np.kron(np.eye(G), np.sin(a3)) * sc
    c["c_k3"] = np.concatenate([K3c, K3s], axis=1).astype(bf)
    c["c_onesb"] = np.ones((128, 128), bf)
    c["c_ident"] = np.eye(128).astype(bf)
    return c


def _tile_w(W, nk, no):
    return np.ascontiguousarray(W.reshape(nk, 128, no, 128).transpose(2, 1, 0, 3)).reshape(no * 128, nk * 128)


def _prep_weights(inp):
    L = inp["w_in"].shape[0]
    out = {}
    for ab, sfx in (("a", "ffa"), ("b", "ffb")):
        wgu = np.empty((L, FC * 128, 2 * KC * 128), np.float32)
        wd = np.empty((L, KC * 128, FC * 128), np.float32)
        for l in range(L):
            g = _tile_w(np.asarray(inp[sfx + "_gate"][l]), KC, FC).reshape(FC, 128, 1, KC * 128)
            u = _tile_w(np.asarray(inp[sfx + "_up"][l]), KC, FC).reshape(FC, 128, 1, KC * 128)
            wgu[l] = np.concatenate([g, u], axis=2).reshape(FC * 128, 2 * KC * 128)
            wd[l] = _tile_w(np.asarray(inp[sfx + "_down"][l]), FC, KC)
        out["wgu_" + ab] = wgu
        out["wd_" + ab] = wd
    out["win"] = np.stack([_tile_w(np.asarray(inp["w_in"][l]), KC, 64) for l in range(L)])
    out["pf"] = np.stack([_tile_w(np.asarray(inp["p_f"][l]), 8, 16) for l in range(L)])
    out["pa"] = np.stack([_tile_w(np.asarray(inp["p_a"][l]), 8, 16) for l in range(L)])
    out["wo"] = np.stack([_tile_w(np.asarray(inp["w_o"][l]), 16, 16) for l in range(L)])
    gains = np.empty((L, 128, 64), np.float32)
    small = np.zeros((L, 128, 4), np.float32)
    lamv = np.empty((L, 128, 256), np.float32)
    for l in range(L):
        for gi, nm in enumerate(("norm_ffa", "norm_mix", "norm_ffb", "norm_out")):
            gains[l, :, gi * 16:(gi + 1) * 16] = np.asarray(inp[nm][l]).reshape(16, 128).T
        small[l, :, 0] = np.tile(np.asarray(inp["q_norm"][l]), 2)
        small[l, :, 1] = np.tile(np.asarray(inp["k_norm"][l]), 2)
        small[l, :, 2] = np.asarray(inp["subln"][l])
        lv = np.concatenate([np.asarray(inp[k][l]) for k in ("lambda_q1", "lambda_k1", "lambda_q2", "lambda_k2")])
        lamv[l] = np.broadcast_to(lv[None, :], (128, 256))
    out["gains"] = gains
    out["small"] = small
    out["lamv"] = lamv
    return out


def kernel(**inputs):
    x = np.asarray(inputs["x"], dtype=np.float32)
    B, S, _ = x.shape
    L = inputs["w_in"].shape[0]
    nc = build_nc(S, L)
    common = _prep_weights(inputs)
    common.update(_consts(S))
    in_maps = [dict(common, xT=np.ascontiguousarray(x[b].T)) for b in range(B)]
    res = run_bass_kernel_spmd(nc, in_maps, core_ids=list(range(B)))
    if DBG:
        kernel.dbg = res.results
    out = np.stack([np.ascontiguousarray(np.asarray(res.results[b]["outT"]).T) for b in range(B)])
    return out.astype(np.float32)
```

```python
import math
from contextlib import ExitStack
import numpy as np
import ml_dtypes
import concourse.bass as bass
import concourse.mybir as mybir
from concourse.bass_utils import run_bass_kernel_spmd

F32 = mybir.dt.float32
BF16 = mybir.dt.bfloat16
ALU = mybir.AluOpType
AF = mybir.ActivationFunctionType

D = 2048
KC = 16
DFF = 5632
FC = 44
NH = 8
T = 512
EPS = 1e-6
ROPE_THETA = 500000.0


class Buf:
    __slots__ = ("w", "r")

    def __init__(self):
        self.w = None
        self.r = []


class BP:
    ENG = ("pe", "act", "dve", "pool", "sp")
    UID = 0
    NBLK = 0
    LVL = 9
    REV = False
    CNT = {}
    SEM = {}
    GST = None
    STOP = 10 ** 9

    def __init__(self, nc):
        self.nc = nc
        self.ops = {e: [] for e in self.ENG}
        self.cnt = BP.CNT
        self.sem = BP.SEM
        self.st = ExitStack()
        self.bufs = set()

    def _sem(self, name):
        if name not in self.sem:
            self.sem[name] = BP.GST.enter_context(self.nc.semaphore(name))
            self.cnt[name] = 0

    def op(self, eng, fn, reads=(), writes=(), dma=None, signal=True):
        deps = []
        for b in reads:
            if b.w is not None:
                deps.append(b.w)
        for b in writes:
            if b.w is not None:
                deps.append(b.w)
            deps.extend(b.r)
        if dma:
            semn, inc = "d_" + dma, 16
        else:
            semn, inc = "c_" + eng, 1
        self._sem(semn)
        if signal:
            self.cnt[semn] += inc
            ev = (semn, self.cnt[semn])
        else:
            ev = (semn, self.cnt[semn] + inc)
        if eng == "pe":
            deps = [d for d in deps if d[0] != "c_pe"]
        self.ops[eng].append((fn, deps, semn if signal else None, inc))
        for b in reads:
            b.r.append(ev)
            self.bufs.add(b)
        for b in writes:
            b.w = ev
            b.r = []
            self.bufs.add(b)
        return ev

    def run(self):
        nc = self.nc
        BP.NBLK += 1
        if BP.NBLK > BP.STOP:
            self.st.close()
            return
        final = dict(self.cnt)

        def mk(en):
            def body(e):
                waited = {}
                for fn, deps, semn, inc in self.ops[en]:
                    need = {}
                    for dn, dv in deps:
                        if dv > need.get(dn, 0):
                            need[dn] = dv
                    for dn, dv in need.items():
                        if waited.get(dn, 0) < dv:
                            e.wait_ge(self.sem[dn], dv)
                            waited[dn] = dv
                    ins = fn(e)
                    if semn is not None:
                        ins.then_inc(self.sem[semn], inc)
                if en == "sp":
                    for dn, dv in final.items():
                        if dv > 0 and waited.get(dn, 0) < dv:
                            e.wait_ge(self.sem[dn], dv)
            return body

        with nc.Block() as blk:
            blk.tensor(mk("pe"))
            blk.scalar(mk("act"))
            blk.vector(mk("dve"))
            blk.gpsimd(mk("pool"))
            blk.sync(mk("sp"))
        for b in self.bufs:
            b.w = None
            b.r = []
        self.st.close()


def lam_init_of(i):
    return 0.8 - 0.6 * math.exp(-0.3 * i)


DBG = False


def build_nc(S, DEPTH):
    N2 = S // 128
    G = 128 // N2
    NGRP = 128 // G
    NT = S // T
    NKT = S // 128
    nc = bass.Bass("TRN2", target_bir_lowering=False)
    BP.CNT = {}
    BP.SEM = {}
    BP.GST = ExitStack()
    BP.NBLK = 0

    def din(name, shape, dt=F32):
        return nc.dram_tensor(name, list(shape), dt, kind="ExternalInput").ap()

    def dscr(name, shape, dt=BF16):
        return nc.dram_tensor(name, list(shape), dt, kind="Internal").ap()

    xT = din("xT", [D, S])
    outT = nc.dram_tensor("outT", [D, S], F32, kind="ExternalOutput").ap()
    wgu_in = [din("wgu_a", [DEPTH, FC * 128, 2 * KC * 128]), din("wgu_b", [DEPTH, FC * 128, 2 * KC * 128])]
    wd_in = [din("wd_a", [DEPTH, KC * 128, FC * 128]), din("wd_b", [DEPTH, KC * 128, FC * 128])]
    win_in = din("win", [DEPTH, 64 * 128, KC * 128])
    pf_in = din("pf", [DEPTH, 16 * 128, 8 * 128])
    pa_in = din("pa", [DEPTH, 16 * 128, 8 * 128])
    wo_in = din("wo", [DEPTH, 16 * 128, 16 * 128])
    gains_in = din("gains", [DEPTH, 128, 4 * 16])
    small_in = din("small", [DEPTH, 128, 4])
    lamv_in = din("lamv", [DEPTH, 128, 256])
    c_ones = din("c_ones", [128, 128])
    c_blk = din("c_blk", [128, 128])
    c_rot = din("c_rot", [128, 128])
    c_cos = din("c_cos", [128, S])
    c_sin = din("c_sin", [128, S])
    c_cs = din("c_cs", [128, 2 * 512], BF16)
    c_m1 = din("c_m1", [128, 512], BF16)
    c_tw = din("c_tw", [128, 2 * N2 * 128], BF16)
    c_k3 = din("c_k3", [128, 256], BF16)
    c_onesb = din("c_onesb", [128, 128], BF16)
    c_ident = din("c_ident", [128, 128], BF16)

    xs = dscr("xs", [D, S], F32)
    wgu_bf = [dscr("wgu_bf0", [FC * 128, 2 * KC * 128]), dscr("wgu_bf1", [FC * 128, 2 * KC * 128])]
    wd_bf = [dscr("wd_bf0", [KC * 128, FC * 128]), dscr("wd_bf1", [KC * 128, FC * 128])]
    win_bf = dscr("win_bf", [64 * 128, KC * 128])
    pf_bf = dscr("pf_bf", [16 * 128, 8 * 128])
    pa_bf = dscr("pa_bf", [16 * 128, 8 * 128])
    wo_bf = dscr("wo_bf", [16 * 128, 16 * 128])
    qT_d = dscr("qT_d", [NH * 128, S])
    kT_d = dscr("kT_d", [NH * 128, S])
    v_d = dscr("v_d", [S, NH * 128])
    ab_d = dscr("ab_d", [S, 4 * 512])
    fT_d = dscr("fT_d", [8 * 128, S])
    oT_d = dscr("oT_d", [NH * 128, S])

    top = ExitStack()

    uid = [0]

    def sb(name, shape, dt=F32, st=top):
        uid[0] += 1
        return st.enter_context(nc.sbuf_tensor(f"s_{name}_{uid[0]}", list(shape), dt))

    PS = [top.enter_context(nc.psum_tensor(f"ps{i}", [128, 512], F32)) for i in range(7)]
    PSB = [Buf() for _ in range(7)]
    PTt = top.enter_context(nc.psum_tensor("pt", [128, 1024], BF16))
    PT = [PTt[:, 0:256], PTt[:, 256:512]]
    PTB = [Buf(), Buf()]

    ones_f = sb("ones_f", [128, 128]); blk_f = sb("blk_f", [128, 128]); rot_f = sb("rot_f", [128, 128])
    onesb = sb("onesb", [128, 128], BF16)
    identb = sb("identb", [128, 128], BF16)
    cs_b = sb("cs_b", [128, 2, 512], BF16)
    m1_b = sb("m1_b", [128, 2, 256], BF16)
    k3_b = sb("k3_b", [128, 2, 128], BF16)
    eps_t = sb("eps_t", [128, 1])
    gains = sb("gains", [128, 4, 16])
    small = sb("small", [128, 4])
    lamv = sb("lamv", [128, 4, 64])
    lamt = sb("lamt", [128, 8])
    CONST = Buf()
    LAYERC = Buf()

    bp = BP(nc)
    bp.op("sp", lambda e: e.dma_start(out=ones_f[:, :], in_=c_ones[:, :]), writes=[CONST], dma="c0")
    bp.op("sp", lambda e: e.dma_start(out=blk_f[:, :], in_=c_blk[:, :]), writes=[CONST], dma="c1")
    bp.op("sp", lambda e: e.dma_start(out=rot_f[:, :], in_=c_rot[:, :]), writes=[CONST], dma="c2")
    bp.op("sp", lambda e: e.dma_start(out=onesb[:, :], in_=c_onesb[:, :]), writes=[CONST], dma="c3")
    bp.op("sp", lambda e: e.dma_start(out=cs_b[:, :, :], in_=c_cs.rearrange("p (k c) -> p k c", k=2)), writes=[CONST], dma="c4")
    bp.op("sp", lambda e: e.dma_start(out=m1_b[:, :, :], in_=c_m1.rearrange("p (k c) -> p k c", k=2)), writes=[CONST], dma="c5")
    bp.op("sp", lambda e: e.dma_start(out=k3_b[:, :, :], in_=c_k3.rearrange("p (k c) -> p k c", k=2)), writes=[CONST], dma="c6")
    bp.op("sp", lambda e: e.dma_start(out=identb[:, :], in_=c_ident[:, :]), writes=[CONST], dma="c7")
    bp.op("dve", lambda e: e.memset(eps_t[:, :], EPS), writes=[CONST])
    for c in range(KC):
        bp.op("pool", lambda e, c=c: e.dma_start(out=xs[c * 128:(c + 1) * 128, :], in_=xT[c * 128:(c + 1) * 128, :]),
              writes=[Buf()], dma=f"x{c % 4}")
    bp.run()

    def cast_rows(bp, dst, src, nrows, tag):
        for i in range(nrows // 128):
            bp.op("pool", lambda e, i=i: e.dma_start(out=dst[i * 128:(i + 1) * 128, :], in_=src[i * 128:(i + 1) * 128, :]),
                  writes=[Buf()], dma=f"cast{i % 4}")

    def rmsnorm(bp, xt, XT, g_idx, tmp, dst_fn, DST, sqb, SQ):
        rt, rstd, RT, RSTD = tmp
        for c in range(KC):
            s = c % 4
            bp.op("act", lambda e, c=c, s=s: e.activation(out=sqb[:, s, :], in_=xt[:, c, :], func=AF.Square),
                  reads=[XT[c]], writes=[SQ[s]])
            bp.op("pe", lambda e, c=c, s=s: e.matmul(PS[6][:, :], lhsT=ones_f[:, :], rhs=sqb[:, s, :],
                                                    start=(c == 0), stop=(c == KC - 1)),
                  reads=[SQ[s], CONST], writes=[PSB[6]])
        bp.op("act", lambda e: e.activation(out=rt[:, :], in_=PS[6][:, :], func=AF.Sqrt, bias=eps_t[:, 0:1], scale=1.0 / D),
              reads=[PSB[6], CONST], writes=[RT])
        bp.op("dve", lambda e: e.reciprocal(out=rstd[:, :], in_=rt[:, :]), reads=[RT], writes=[RSTD])
        for c in range(KC):
            bp.op("dve", lambda e, c=c: e.scalar_tensor_tensor(out=dst_fn(c), in0=xt[:, c, :], scalar=gains[:, g_idx, c:c + 1],
                                                              in1=rstd[:, :], op0=ALU.mult, op1=ALU.mult),
                  reads=[XT[c], RSTD, LAYERC], writes=[DST[c]])

    def ffn(bp, xt, XT, ht, HT, at, AT, wgu, WGU, wd, WD, sg, SG, wgu_src, wd_src):
        for fc in range(FC):
            s = fc % 2
            bp.op("sp", lambda e, fc=fc, s=s: e.dma_start(
                out=wgu[s][:, :, :, :], in_=wgu_src[fc * 128:(fc + 1) * 128, :].rearrange("p (g k f) -> p g k f", g=2, k=KC)),
                writes=[WGU[s]], dma=f"wgu{s}")
            pg, pu = 2 * s, 2 * s + 1
            for kc in range(KC):
                bp.op("pe", lambda e, kc=kc, s=s, pg=pg: e.matmul(PS[pg][:, :], lhsT=wgu[s][:, 0, kc, :], rhs=ht[:, kc, :],
                                                                 start=(kc == 0), stop=(kc == KC - 1)),
                      reads=[WGU[s], HT[kc]], writes=[PSB[pg]], signal=(kc == KC - 1))
            for kc in range(KC):
                bp.op("pe", lambda e, kc=kc, s=s, pu=pu: e.matmul(PS[pu][:, :], lhsT=wgu[s][:, 1, kc, :], rhs=ht[:, kc, :],
                                                                 start=(kc == 0), stop=(kc == KC - 1)),
                      reads=[WGU[s], HT[kc]], writes=[PSB[pu]], signal=(kc == KC - 1))
            bp.op("act", lambda e, s=s, pg=pg: e.activation(out=sg[s][:, :], in_=PS[pg][:, :], func=AF.Silu),
                  reads=[PSB[pg]], writes=[SG[s]])
            bp.op("dve", lambda e, fc=fc, s=s, pu=pu: e.tensor_tensor(out=at[:, fc, :], in0=sg[s][:, :], in1=PS[pu][:, :], op=ALU.mult),
                  reads=[SG[s], PSB[pu]], writes=[AT[fc]])
        for dc in range(KC):
            s = dc % 2
            bp.op("sp", lambda e, dc=dc, s=s: e.dma_start(
                out=wd[s][:, :, :], in_=wd_src[dc * 128:(dc + 1) * 128, :].rearrange("p (k f) -> p k f", k=FC)),
                writes=[WD[s]], dma=f"wd{s}")
            py = 4 + s
            for fc in range(FC):
                bp.op("pe", lambda e, fc=fc, s=s, py=py: e.matmul(PS[py][:, :], lhsT=wd[s][:, fc, :], rhs=at[:, fc, :],
                                                                 start=(fc == 0), stop=(fc == FC - 1)),
                      reads=[WD[s], AT[fc]], writes=[PSB[py]], signal=(fc == FC - 1))
            bp.op("dve", lambda e, dc=dc, py=py: e.scalar_tensor_tensor(out=xt[:, dc, :], in0=PS[py][:, :], scalar=0.5,
                                                                       in1=xt[:, dc, :], op0=ALU.mult, op1=ALU.add),
                  reads=[PSB[py], XT[dc]], writes=[XT[dc]])

    def proj(bp, ws, WS, slot, src_rows, nk, rhs_fn, RHS, pbank, tag):
        bp.op("sp", lambda e: e.dma_start(out=ws[slot][:, 0:nk, :], in_=src_rows.rearrange("p (k f) -> p k f", k=nk)),
              writes=[WS[slot]], dma=f"{tag}{slot}")
        for kc in range(nk):
            bp.op("pe", lambda e, kc=kc: e.matmul(PS[pbank][:, :], lhsT=ws[slot][:, kc, :], rhs=rhs_fn(kc),
                                                  start=(kc == 0), stop=(kc == nk - 1)),
                  reads=[WS[slot], RHS[kc]], writes=[PSB[pbank]], signal=(kc == nk - 1))

    for l in range(DEPTH):
        li = lam_init_of(l)
        bp = BP(nc)
        for ab in range(2):
            cast_rows(bp, wgu_bf[ab], wgu_in[ab][l], FC * 128, f"g{ab}")
            cast_rows(bp, wd_bf[ab], wd_in[ab][l], KC * 128, f"d{ab}")
        cast_rows(bp, win_bf, win_in[l], 64 * 128, "wi")
        cast_rows(bp, pf_bf, pf_in[l], 16 * 128, "pf")
        cast_rows(bp, pa_bf, pa_in[l], 16 * 128, "pa")
        cast_rows(bp, wo_bf, wo_in[l], 16 * 128, "wo")
        bp.op("sp", lambda e, l=l: e.dma_start(out=gains[:, :, :], in_=gains_in[l].rearrange("p (g c) -> p g c", g=4)),
              writes=[LAYERC], dma="l0")
        bp.op("sp", lambda e, l=l: e.dma_start(out=small[:, :], in_=small_in[l]), writes=[LAYERC], dma="l1")
        LV = Buf()
        bp.op("sp", lambda e, l=l: e.dma_start(out=lamv[:, :, :], in_=lamv_in[l].rearrange("p (g c) -> p g c", g=4)),
              writes=[LV], dma="l2")
        LT = Buf()
        bp.op("dve", lambda e: e.tensor_tensor(out=lamv[:, 0, :], in0=lamv[:, 0, :], in1=lamv[:, 1, :], op=ALU.mult), reads=[LV], writes=[LV])
        bp.op("dve", lambda e: e.tensor_tensor(out=lamv[:, 2, :], in0=lamv[:, 2, :], in1=lamv[:, 3, :], op=ALU.mult), reads=[LV], writes=[LV])
        bp.op("dve", lambda e: e.tensor_reduce(out=lamt[:, 0:1], in_=lamv[:, 0, :], axis=mybir.AxisListType.X, op=ALU.add), reads=[LV], writes=[LT])
        bp.op("dve", lambda e: e.tensor_reduce(out=lamt[:, 1:2], in_=lamv[:, 2, :], axis=mybir.AxisListType.X, op=ALU.add), reads=[LT, LV], writes=[LT])
        bp.op("act", lambda e: e.activation(out=lamt[:, 2:4], in_=lamt[:, 0:2], func=AF.Exp), reads=[LT], writes=[LT])
        bp.op("dve", lambda e, li=li: e.scalar_tensor_tensor(out=lamt[:, 4:5], in0=lamt[:, 2:3], scalar=li, in1=lamt[:, 3:4],
                                                             op0=ALU.add, op1=ALU.subtract), reads=[LT], writes=[LT])
        bp.op("dve", lambda e: e.tensor_scalar(out=lamt[:, 5:6], in0=lamt[:, 4:5], scalar1=-1.0, scalar2=None, op0=ALU.mult),
              reads=[LT], writes=[LT])
        bp.op("dve", lambda e, li=li: e.tensor_scalar(out=small[:, 3:4], in0=small[:, 2:3], scalar1=(1.0 - li), scalar2=None, op0=ALU.mult),
              reads=[LAYERC], writes=[LAYERC])
        bp.run()

        for phase in ("A", "C"):
            st = ExitStack()
            xt = sb("xt", [128, KC, T], F32, st)
            ht = sb("ht", [128, KC, T], BF16, st)
            big = sb("big", [128, FC * T], BF16, st)
            at = big[:, :].rearrange("p (k t) -> p k t", k=FC)
            wgu = [sb(f"wgu{i}", [128, 2, KC, 128], BF16, st) for i in range(2)]
            wd = [sb(f"wd{i}", [128, FC, 128], BF16, st) for i in range(2)]
            ws = [sb(f"ws{i}", [128, KC, 128], BF16, st) for i in range(2)]
            sqb = sb("sqb", [128, 4, T], F32, st)
            sg = [sb(f"sg{i}", [128, T], F32, st) for i in range(2)]
            tmp = [sb(f"tmp{i}", [128, T], F32, st) for i in range(8)]
            qo = [sb(f"qo{i}", [128, T], BF16, st) for i in range(2)]
            cst = sb("cst", [128, 2, T], F32, st)
            XT = [Buf() for _ in range(KC)]; HT = [Buf() for _ in range(KC)]; AT = [Buf() for _ in range(FC)]
            WGU = [Buf(), Buf()]; WD = [Buf(), Buf()]; WS = [Buf(), Buf()]; SQ = [Buf() for _ in range(4)]
            SG = [Buf(), Buf()]; TMP = [Buf() for _ in range(8)]; QO = [Buf(), Buf()]; CST = Buf()
            ut = big[:, 0:8 * T].rearrange("p (k t) -> p k t", k=8)
            vo = big[:, 8 * T:16 * T].rearrange("p (s c) -> p s c", s=4)
            abo = big[:, 16 * T:32 * T].rearrange("p (s g c) -> p s g c", s=4, g=4)
            UT = [Buf() for _ in range(8)]; VO = Buf(); ABO = Buf()
            ftt = big[:, 0:8 * T].rearrange("p (k t) -> p k t", k=8)
            ott = big[:, 8 * T:16 * T].rearrange("p (k t) -> p k t", k=8)
            mt = big[:, 16 * T:32 * T].rearrange("p (k t) -> p k t", k=16)
            FTT = [Buf() for _ in range(8)]; OTT = [Buf() for _ in range(8)]; MT = [Buf() for _ in range(16)]

            for ti in (range(NT) if (phase == "A" or not BP.REV) else reversed(range(NT))):
                t0 = ti * T
                bp = BP(nc)
                bp.op("sp", lambda e: e.dma_start(out=xt[:, :, :], in_=xs[:, t0:t0 + T].rearrange("(c p) t -> p c t", p=128)),
                      writes=XT, dma="xt")
                norm_tmp = (tmp[0], tmp[1], TMP[0], TMP[1])
                if phase == "A":
                    rmsnorm(bp, xt, XT, 0, norm_tmp, lambda c: ht[:, c, :], HT, sqb, SQ)
                    ffn(bp, xt, XT, ht, HT, at, AT, wgu, WGU, wd, WD, sg, SG, wgu_bf[0], wd_bf[0])
                    bp.op("pool", lambda e: e.dma_start(out=xs[:, t0:t0 + T].rearrange("(c p) t -> p c t", p=128), in_=xt[:, :, :]),
                          reads=XT, dma="xo")
                    rmsnorm(bp, xt, XT, 1, norm_tmp, lambda c: ht[:, c, :], HT, sqb, SQ)
                    bp.op("sp", lambda e: e.dma_start(out=cst[:, 0, :], in_=c_cos[:, t0:t0 + T]), writes=[CST], dma="cs0")
                    bp.op("sp", lambda e: e.dma_start(out=cst[:, 1, :], in_=c_sin[:, t0:t0 + T]), writes=[CST], dma="cs1")
                    n = 0
                    for oc in range(8):
                        s = n % 2; pb = n % 2; n += 1
                        proj(bp, ws, WS, s, win_bf[oc * 128:(oc + 1) * 128, :], KC, lambda kc: ht[:, kc, :], HT, pb, "ws")
                        bp.op("act", lambda e, oc=oc, pb=pb: e.activation(out=ut[:, oc, :], in_=PS[pb][:, :], func=AF.Copy),
                              reads=[PSB[pb]], writes=[UT[oc]])
                    for oc in range(8, 24):
                        s = n % 2; pb = n % 2; n += 1
                        isq = oc < 16
                        hd = oc - 8 if isq else oc - 16
                        dstd = qT_d if isq else kT_d
                        gcol = 0 if isq else 1
                        proj(bp, ws, WS, s, win_bf[oc * 128:(oc + 1) * 128, :], KC, lambda kc: ht[:, kc, :], HT, pb, "ws")
                        bp.op("act", lambda e, pb=pb: e.activation(out=tmp[2][:, :], in_=PS[pb][:, :], func=AF.Square),
                              reads=[PSB[pb]], writes=[TMP[2]])
                        bp.op("pe", lambda e: e.matmul(PS[2][:, :], lhsT=blk_f[:, :], rhs=tmp[2][:, :], start=True, stop=True),
                              reads=[TMP[2], CONST], writes=[PSB[2]])
                        bp.op("act", lambda e: e.activation(out=tmp[3][:, :], in_=PS[2][:, :], func=AF.Sqrt, bias=eps_t[:, 0:1], scale=1.0 / 64),
                              reads=[PSB[2]], writes=[TMP[3]])
                        bp.op("dve", lambda e: e.reciprocal(out=tmp[4][:, :], in_=tmp[3][:, :]), reads=[TMP[3]], writes=[TMP[4]])
                        bp.op("dve", lambda e, pb=pb, gcol=gcol: e.scalar_tensor_tensor(
                            out=tmp[5][:, :], in0=PS[pb][:, :], scalar=small[:, gcol:gcol + 1], in1=tmp[4][:, :], op0=ALU.mult, op1=ALU.mult),
                            reads=[PSB[pb], TMP[4], LAYERC], writes=[TMP[5]])
                        bp.op("pe", lambda e: e.matmul(PS[3][:, :], lhsT=rot_f[:, :], rhs=tmp[5][:, :], start=True, stop=True),
                              reads=[TMP[5], CONST], writes=[PSB[3]])
                        bp.op("dve", lambda e: e.tensor_tensor(out=tmp[6][:, :], in0=tmp[5][:, :], in1=cst[:, 0, :], op=ALU.mult),
                              reads=[TMP[5], CST], writes=[TMP[6]])
                        bp.op("dve", lambda e: e.tensor_tensor(out=tmp[7][:, :], in0=PS[3][:, :], in1=cst[:, 1, :], op=ALU.mult),
                              reads=[PSB[3], CST], writes=[TMP[7]])
                        q = hd % 2
                        bp.op("pool", lambda e, q=q: e.tensor_tensor(out=qo[q][:, :], in0=tmp[6][:, :], in1=tmp[7][:, :], op=ALU.add),
                              reads=[TMP[6], TMP[7]], writes=[QO[q]])
                        bp.op("pool", lambda e, q=q, hd=hd, dstd=dstd: e.dma_start(out=dstd[hd * 128:(hd + 1) * 128, t0:t0 + T], in_=qo[q][:, :]),
                              reads=[QO[q]], dma=f"qo{q}")
                    for oc in range(24, 32):
                        s = n % 2; n += 1
                        bp.op("sp", lambda e, oc=oc, s=s: e.dma_start(
                            out=ws[s][:, :, :], in_=win_bf[oc * 128:(oc + 1) * 128, :].rearrange("p (k f) -> p k f", k=KC)),
                            writes=[WS[s]], dma=f"ws{s}")
                        for sub in range(4):
                            pb = 4 + (sub % 2)
                            for kc in range(KC):
                                bp.op("pe", lambda e, kc=kc, sub=sub, s=s, pb=pb: e.matmul(
                                    PS[pb][:, 0:128], lhsT=ht[:, kc, sub * 128:(sub + 1) * 128], rhs=ws[s][:, kc, :],
                                    start=(kc == 0), stop=(kc == KC - 1)),
                                    reads=[WS[s], HT[kc]], writes=[PSB[pb]], signal=(kc == KC - 1))
                            bp.op("act", lambda e, sub=sub, oc=oc, pb=pb: e.activation(
                                out=vo[:, sub, (oc - 24) * 128:(oc - 23) * 128], in_=PS[pb][:, 0:128], func=AF.Copy),
                                reads=[PSB[pb]], writes=[VO])
                    bp.op("pool", lambda e: e.dma_start(out=v_d[t0:t0 + T, :].rearrange("(s p) c -> p s c", p=128), in_=vo),
                          reads=[VO], dma="vo")
                    for sub in range(4):
                        for g in range(4):
                            pb = 4 + (g % 2)
                            for kc in range(2):
                                bp.op("pe", lambda e, kc=kc, sub=sub, g=g, pb=pb: e.matmul(
                                    PS[pb][:, :], lhsT=ut[:, 2 * g + kc, sub * 128:(sub + 1) * 128], rhs=cs_b[:, kc, :],
                                    start=(kc == 0), stop=(kc == 1)),
                                    reads=[UT[2 * g + kc], CONST], writes=[PSB[pb]], signal=(kc == 1))
                            bp.op("dve", lambda e, sub=sub, g=g, pb=pb: e.tensor_copy(out=abo[:, sub, g, :], in_=PS[pb][:, :]),
                                  reads=[PSB[pb]], writes=[ABO])
                    bp.op("pool", lambda e: e.dma_start(out=ab_d[t0:t0 + T, :].rearrange("(s p) c -> p s c", p=128),
                                                        in_=abo.rearrange("p s g c -> p s (g c)")),
                          reads=[ABO], dma="abo")
                else:
                    rmsnorm(bp, xt, XT, 1, norm_tmp, lambda c: ht[:, c, :], HT, sqb, SQ)
                    bp.op("sp", lambda e: e.dma_start(out=ftt, in_=fT_d[:, t0:t0 + T].rearrange("(c p) t -> p c t", p=128)),
                          writes=FTT, dma="ft")
                    bp.op("sp", lambda e: e.dma_start(out=ott, in_=oT_d[:, t0:t0 + T].rearrange("(c p) t -> p c t", p=128)),
                          writes=OTT, dma="ot")
                    for dc in range(16):
                        proj(bp, ws, WS, 0, win_bf[(32 + dc) * 128:(33 + dc) * 128, :], KC, lambda kc: ht[:, kc, :], HT, 0, "ws")
                        bp.op("act", lambda e: e.activation(out=tmp[2][:, :], in_=PS[0][:, :], func=AF.Sigmoid), reads=[PSB[0]], writes=[TMP[2]])
                        proj(bp, ws, WS, 1, win_bf[(48 + dc) * 128:(49 + dc) * 128, :], KC, lambda kc: ht[:, kc, :], HT, 1, "ws")
                        bp.op("act", lambda e: e.activation(out=tmp[3][:, :], in_=PS[1][:, :], func=AF.Sigmoid), reads=[PSB[1]], writes=[TMP[3]])
                        proj(bp, ws, WS, 0, pf_bf[dc * 128:(dc + 1) * 128, :], 8, lambda kc: ftt[:, kc, :], FTT, 2, "ws")
                        proj(bp, ws, WS, 1, pa_bf[dc * 128:(dc + 1) * 128, :], 8, lambda kc: ott[:, kc, :], OTT, 3, "ws")
                        bp.op("dve", lambda e: e.tensor_tensor(out=tmp[4][:, :], in0=tmp[2][:, :], in1=PS[2][:, :], op=ALU.mult),
                              reads=[TMP[2], PSB[2]], writes=[TMP[4]])
                        bp.op("dve", lambda e: e.tensor_tensor(out=tmp[5][:, :], in0=tmp[3][:, :], in1=PS[3][:, :], op=ALU.mult),
                              reads=[TMP[3], PSB[3]], writes=[TMP[5]])
                        bp.op("pool", lambda e, dc=dc: e.tensor_tensor(out=mt[:, dc, :], in0=tmp[4][:, :], in1=tmp[5][:, :], op=ALU.add),
                              reads=[TMP[4], TMP[5]], writes=[MT[dc]])
                    for dc in range(16):
                        s = dc % 2
                        proj(bp, ws, WS, s, wo_bf[dc * 128:(dc + 1) * 128, :], KC, lambda kc: mt[:, kc, :], MT, 4 + s, "ws")
                        bp.op("dve", lambda e, dc=dc, s=s: e.tensor_tensor(out=xt[:, dc, :], in0=xt[:, dc, :], in1=PS[4 + s][:, :], op=ALU.add),
                              reads=[PSB[4 + s], XT[dc]], writes=[XT[dc]])
                    rmsnorm(bp, xt, XT, 2, norm_tmp, lambda c: ht[:, c, :], HT, sqb, SQ)
                    ffn(bp, xt, XT, ht, HT, at, AT, wgu, WGU, wd, WD, sg, SG, wgu_bf[1], wd_bf[1])
                    rmsnorm(bp, xt, XT, 3, norm_tmp, lambda c: xt[:, c, :], XT, sqb, SQ)
                    dst = outT if l == DEPTH - 1 else xs
                    bp.op("pool", lambda e, dst=dst: e.dma_start(out=dst[:, t0:t0 + T].rearrange("(c p) t -> p c t", p=128), in_=xt[:, :, :]),
                          reads=XT, dma="xo")
                bp.run()
            st.close()

            if phase == "C":
                continue
            st = ExitStack()
            NPR = 8
            LOOK = 4
            SBK = [0, 1, 2, 5, 6]
            acc = [sb(f"acc{i}", [128, T], F32, st) for i in range(2)]
            ACC = [Buf(), Buf()]
            accp = [sb(f"accp{i}", [128, T], F32, st) for i in range(2)]
            ACCP = [Buf(), Buf()]
            PSTEP = 4 if NKT >= 8 else 0
            qs_ = sb("qs_", [128, S], BF16, st); ks_ = sb("ks_", [128, S], BF16, st)
            vs_ = sb("vs_", [128, NKT, 128], BF16, st)
            pr = [sb(f"pr{i}", [128, T], BF16, st) for i in range(NPR)]
            fo = [sb(f"fo{i}", [128, T], F32, st) for i in range(4)]
            ft = [sb(f"ft{i}", [128, T], F32, st) for i in range(4)]
            oo = [sb(f"oo{i}", [128, T], BF16, st) for i in range(2)]
            QS = Buf(); KS = Buf(); VS = Buf(); PR = [Buf() for _ in range(NPR)]
            FO = [Buf() for _ in range(4)]; FTB = [Buf() for _ in range(4)]; OO = [Buf(), Buf()]
            for h in range(NH):
                bp = BP(nc)
                bp.op("sp", lambda e, h=h: e.dma_start(out=qs_[:, :], in_=qT_d[h * 128:(h + 1) * 128, :]), writes=[QS], dma="q")
                bp.op("sp", lambda e, h=h: e.dma_start(out=ks_[:, :], in_=kT_d[h * 128:(h + 1) * 128, :]), writes=[KS], dma="k")
                bp.op("sp", lambda e, h=h: e.dma_start(out=vs_[:, :, :], in_=v_d[:, h * 128:(h + 1) * 128].rearrange("(k p) c -> p k c", p=128)),
                      writes=[VS], dma="v")
                units = [(qc, kt, cp) for qc in range(NT) for kt in range(NKT) for cp in range(2)]
                U = len(units)
                ctr = [0]

                def emit_qk(u):
                    qc, kt, cp = units[u]
                    q0 = qc * T
                    pb = SBK[ctr[0] % 5]
                    ctr[0] += 1
                    r = u % NPR
                    bp.op("pe", lambda e, kt=kt, cp=cp, pb=pb, q0=q0: e.matmul(
                        PS[pb][:, :], lhsT=ks_[cp * 64:(cp + 1) * 64, kt * 128:(kt + 1) * 128],
                        rhs=qs_[cp * 64:(cp + 1) * 64, q0:q0 + T], start=True, stop=True),
                        reads=[KS, QS], writes=[PSB[pb]])
                    bp.op("act", lambda e, pb=pb, r=r: e.activation(out=pr[r][:, :], in_=PS[pb][:, :], func=AF.Exp, scale=0.125),
                          reads=[PSB[pb]], writes=[PR[r]])

                def fin1(qc):
                    bp.op("act", lambda e: e.activation(out=fo[0][:, :], in_=PS[3][:, :], func=AF.Copy), reads=[PSB[3]], writes=[FO[0]])
                    bp.op("dve", lambda e: e.tensor_copy(out=fo[1][:, :], in_=PS[4][:, :]), reads=[PSB[4]], writes=[FO[1]])
                    b0 = SBK[ctr[0] % 5]
                    b1 = SBK[(ctr[0] + 1) % 5]
                    ctr[0] += 2
                    for cpp, bb in ((0, b0), (1, b1)):
                        bp.op("pe", lambda e, bb=bb, cpp=cpp: e.matmul(PS[bb][:, :], lhsT=ones_f[:, :], rhs=acc[cpp][:, :], start=True, stop=(PSTEP == 0)),
                              reads=[ACC[cpp], CONST], writes=[PSB[bb]])
                        if PSTEP:
                            bp.op("pe", lambda e, bb=bb, cpp=cpp: e.matmul(PS[bb][:, :], lhsT=ones_f[:, :], rhs=accp[cpp][:, :], start=False, stop=True),
                                  reads=[ACCP[cpp], CONST], writes=[PSB[bb]])
                    bp.op("dve", lambda e, b0=b0: e.reciprocal(out=ft[0][:, :], in_=PS[b0][:, :]), reads=[PSB[b0]], writes=[FTB[0]])
                    bp.op("dve", lambda e, b1=b1: e.reciprocal(out=ft[1][:, :], in_=PS[b1][:, :]), reads=[PSB[b1]], writes=[FTB[1]])
                    bp.op("pool", lambda e: e.tensor_tensor(out=ft[0][:, :], in0=ft[0][:, :], in1=fo[0][:, :], op=ALU.mult),
                          reads=[FTB[0], FO[0]], writes=[FTB[0]])
                    bp.op("dve", lambda e: e.tensor_tensor(out=ft[1][:, :], in0=ft[1][:, :], in1=fo[1][:, :], op=ALU.mult),
                          reads=[FTB[1], FO[1]], writes=[FTB[1]])
                    bp.op("dve", lambda e: e.scalar_tensor_tensor(out=ft[2][:, :], in0=ft[1][:, :], scalar=lamt[:, 5:6], in1=ft[0][:, :],
                                                                  op0=ALU.mult, op1=ALU.add),
                          reads=[FTB[1], FTB[0]], writes=[FTB[2]])
                    bp.op("act", lambda e: e.activation(out=ft[3][:, :], in_=ft[2][:, :], func=AF.Square), reads=[FTB[2]], writes=[FTB[3]])

                def fin2(qc):
                    q0 = qc * T
                    pb = SBK[ctr[0] % 5]
                    ctr[0] += 1
                    bp.op("pe", lambda e, pb=pb: e.matmul(PS[pb][:, :], lhsT=ones_f[:, :], rhs=ft[3][:, :], start=True, stop=True),
                          reads=[FTB[3], CONST], writes=[PSB[pb]])
                    bp.op("act", lambda e, pb=pb: e.activation(out=ft[0][:, :], in_=PS[pb][:, :], func=AF.Sqrt, bias=eps_t[:, 0:1], scale=1.0 / 128),
                          reads=[PSB[pb]], writes=[FTB[0]])
                    bp.op("dve", lambda e: e.reciprocal(out=ft[1][:, :], in_=ft[0][:, :]), reads=[FTB[0]], writes=[FTB[1]])
                    o = qc % 2
                    bp.op("dve", lambda e, o=o: e.scalar_tensor_tensor(out=oo[o][:, :], in0=ft[2][:, :], scalar=small[:, 3:4], in1=ft[1][:, :],
                                                                       op0=ALU.mult, op1=ALU.mult),
                          reads=[FTB[2], FTB[1]], writes=[OO[o]])
                    bp.op("pool", lambda e, o=o, h=h, q0=q0: e.dma_start(out=oT_d[h * 128:(h + 1) * 128, q0:q0 + T], in_=oo[o][:, :]),
                          reads=[OO[o]], dma=f"oo{o}")

                for u in range(min(LOOK, U)):
                    emit_qk(u)
                pending = None
                since = 0
                for u0 in range(0, U, 2):
                    for u in (u0, u0 + 1):
                        qc, kt, cp = units[u]
                        r = u % NPR
                        bp.op("pe", lambda e, kt=kt, cp=cp, r=r: e.matmul(
                            PS[3 + cp][:, :], lhsT=vs_[:, kt, :], rhs=pr[r][:, :], start=(kt == 0), stop=(kt == NKT - 1)),
                            reads=[VS, PR[r]], writes=[PSB[3 + cp]])
                        onpool = PSTEP and (kt % PSTEP == PSTEP - 1)
                        first = (kt == PSTEP - 1) if onpool else (kt == 0)
                        eng = "pool" if onpool else "dve"
                        a_t, A_B = (accp, ACCP) if onpool else (acc, ACC)
                        if first:
                            bp.op(eng, lambda e, cp=cp, r=r, a_t=a_t: e.tensor_copy(out=a_t[cp][:, :], in_=pr[r][:, :]),
                                  reads=[PR[r]], writes=[A_B[cp]])
                        else:
                            bp.op(eng, lambda e, cp=cp, r=r, a_t=a_t: e.tensor_tensor(out=a_t[cp][:, :], in0=a_t[cp][:, :], in1=pr[r][:, :], op=ALU.add),
                                  reads=[PR[r], A_B[cp]], writes=[A_B[cp]])
                    for u in (u0, u0 + 1):
                        if u + LOOK < U:
                            emit_qk(u + LOOK)
                    since += 2
                    qc, kt, cp = units[u0 + 1]
                    if pending is not None and since >= 12:
                        fin2(pending)
                        pending = None
                    if kt == NKT - 1:
                        if pending is not None:
                            fin2(pending)
                        fin1(qc)
                        pending = qc
                        since = 0
                if pending is not None:
                    fin2(pending)
                bp.run()
            st.close()

            st = ExitStack()
            xsb = sb("xsb", [128, N2, 256], BF16, st)
            tw = sb("tw", [128, 2, N2, 128], BF16, st)
            yp = sb("yp", [128, 256, N2], BF16, st)
            fts = sb("fts", [128, S], BF16, st)
            fq = [sb(f"fq{i}", [128, 256], F32, st) for i in range(4)]
            yt = [sb(f"yt{i}", [128, 2, 128], BF16, st) for i in range(2)]
            XSB = Buf(); TW = Buf(); YP = [Buf() for _ in range(N2)]; FTS = Buf(); FQ = [Buf() for _ in range(4)]; YT = [Buf(), Buf()]
            for j in range(8):
                g, hf = j // 2, j % 2
                bp = BP(nc)
                bp.op("sp", lambda e: e.dma_start(out=tw[:, :, :, :], in_=c_tw.rearrange("p (a s t) -> p a s t", a=2, s=N2)), writes=[TW], dma="tw")
                bp.op("sp", lambda e, g=g, hf=hf: e.dma_start(
                    out=xsb[:, :, :], in_=ab_d[:, g * 512 + hf * 256: g * 512 + hf * 256 + 256].rearrange("(p s) c -> p s c", s=N2)),
                    writes=[XSB], dma="xsb")
                for s2 in range(N2 if BP.LVL >= 2 else 0):
                    pb = s2 % 2
                    bp.op("pe", lambda e, s2=s2, pb=pb: e.matmul(PS[pb][:, 0:256], lhsT=xsb[:, s2, 0:128], rhs=m1_b[:, 0, :], start=True, stop=False),
                          reads=[XSB, CONST], writes=[PSB[pb]], signal=False)
                    bp.op("pe", lambda e, s2=s2, pb=pb: e.matmul(PS[pb][:, 0:256], lhsT=xsb[:, s2, 128:256], rhs=m1_b[:, 1, :], start=False, stop=True),
                          reads=[XSB, CONST], writes=[PSB[pb]])
                    a, b = 2 * (s2 % 2), 2 * (s2 % 2) + 1
                    bp.op("dve", lambda e, s2=s2, pb=pb, a=a: e.tensor_tensor(out=fq[a][:, 0:128], in0=PS[pb][:, 0:128], in1=tw[:, 0, s2, :], op=ALU.mult),
                          reads=[PSB[pb], TW], writes=[FQ[a]])
                    bp.op("dve", lambda e, s2=s2, pb=pb, a=a: e.tensor_tensor(out=fq[a][:, 128:256], in0=PS[pb][:, 128:256], in1=tw[:, 0, s2, :], op=ALU.mult),
                          reads=[PSB[pb], TW, FQ[a]], writes=[FQ[a]])
                    bp.op("dve", lambda e, s2=s2, pb=pb, b=b: e.tensor_tensor(out=fq[b][:, 0:128], in0=PS[pb][:, 128:256], in1=tw[:, 1, s2, :], op=ALU.mult),
                          reads=[PSB[pb], TW], writes=[FQ[b]])
                    bp.op("dve", lambda e, s2=s2, pb=pb, b=b: e.tensor_tensor(out=fq[b][:, 128:256], in0=PS[pb][:, 0:128], in1=tw[:, 1, s2, :], op=ALU.mult),
                          reads=[PSB[pb], TW, FQ[b]], writes=[FQ[b]])
                    bp.op("pool", lambda e, s2=s2, a=a, b=b: e.tensor_tensor(out=yp[:, 0:128, s2], in0=fq[a][:, 0:128], in1=fq[b][:, 0:128], op=ALU.add),
                          reads=[FQ[a], FQ[b]], writes=[YP[s2]])
                    bp.op("pool", lambda e, s2=s2, a=a, b=b: e.tensor_tensor(out=yp[:, 128:256, s2], in0=fq[a][:, 128:256], in1=fq[b][:, 128:256], op=ALU.subtract),
                          reads=[FQ[a], FQ[b], YP[s2]], writes=[YP[s2]])
                for gp in range(NGRP if BP.LVL >= 3 else 0):
                    y = gp % 2
                    for ri in range(2):
                        src = yp[:, ri * 128 + gp * G: ri * 128 + gp * G + G, :].rearrange("p t s -> p (t s)")
                        bp.op("pe", lambda e, src=src, ri=ri, y=y: e.transpose(out=PT[y][:, ri * 128:(ri + 1) * 128], in_=src, identity=identb[:, :]),
                              reads=YP + [CONST], writes=[PTB[y]])
                    if BP.LVL < 4:
                        continue
                    bp.op("act", lambda e, y=y: e.activation(out=yt[y][:, :, :].rearrange("p a c -> p (a c)"), in_=PT[y], func=AF.Copy),
                          reads=[PTB[y]], writes=[YT[y]])
                    pb = 4 + y
                    bp.op("pe", lambda e, y=y, pb=pb: e.matmul(PS[pb][:, 0:128], lhsT=yt[y][:, 0, :], rhs=k3_b[:, 0, :], start=True, stop=False),
                          reads=[YT[y], CONST], writes=[PSB[pb]], signal=False)
                    bp.op("pe", lambda e, y=y, pb=pb: e.matmul(PS[pb][:, 0:128], lhsT=yt[y][:, 1, :], rhs=k3_b[:, 1, :], start=False, stop=True),
                          reads=[YT[y], CONST], writes=[PSB[pb]])
                    dstv = fts[:, :].rearrange("p (t2 t1) -> p t1 t2", t1=128)[:, gp * G:(gp + 1) * G, :]
                    bp.op("dve", lambda e, pb=pb, dstv=dstv: e.tensor_copy(out=dstv, in_=PS[pb][:, 0:128].rearrange("p (a b) -> p a b", a=G)),
                          reads=[PSB[pb]], writes=[FTS])
                bp.op("pool", lambda e, j=j: e.dma_start(out=fT_d[j * 128:(j + 1) * 128, :], in_=fts[:, :]), reads=[FTS], dma="fts")
                bp.run()
            st.close()
            if DBG and l == 0:
                bp = BP(nc)
                for nm, src, dt in (("q", qT_d, BF16), ("k", kT_d, BF16), ("o", oT_d, BF16), ("f", fT_d, BF16)):
                    dd = nc.dram_tensor("dbg_" + nm, [1024, S], dt, kind="ExternalOutput").ap()
                    bp.op("pool", lambda e, dd=dd, src=src: e.dma_start(out=dd[:, :], in_=src[:, :]), writes=[Buf()], dma="dbg" + nm)
                dd = nc.dram_tensor("dbg_x1", [D, S], F32, kind="ExternalOutput").ap()
                bp.op("pool", lambda e, dd=dd: e.dma_start(out=dd[:, :], in_=xs[:, :]), writes=[Buf()], dma="dbgx")
                dd = nc.dram_tensor("dbg_v", [S, 1024], BF16, kind="ExternalOutput").ap()
                bp.op("pool", lambda e, dd=dd: e.dma_start(out=dd[:, :], in_=v_d[:, :]), writes=[Buf()], dma="dbgv")
                dd = nc.dram_tensor("dbg_ab", [S, 2048], BF16, kind="ExternalOutput").ap()
                bp.op("pool", lambda e, dd=dd: e.dma_start(out=dd[:, :], in_=ab_d[:, :]), writes=[Buf()], dma="dbgab")
                bp.run()
    top.close()
    BP.GST.close()
    return nc


def _consts(S):
    N2 = S // 128
    G = 128 // N2
    bf = ml_dtypes.bfloat16
    c = {}
    c["c_ones"] = np.ones((128, 128), np.float32)
    blk = np.zeros((128, 128), np.float32)
    blk[:64, :64] = 1.0
    blk[64:, 64:] = 1.0
    c["c_blk"] = blk
    rot = np.zeros((128, 128), np.float32)
    for base in (0, 64):
        for i in range(8):
            rot[base + i + 8, base + i] = -1.0
            rot[base + i, base + i + 8] = 1.0
    c["c_rot"] = rot
    pos = np.arange(S, dtype=np.float32)
    inv = (np.float32(ROPE_THETA) ** (-(np.arange(0, 16, 2, dtype=np.float32)) / np.float32(16))).astype(np.float32)
    ang = (pos[None, :] * inv[:, None]).astype(np.float32)
    cosT = np.ones((128, S), np.float32)
    sinT = np.zeros((128, S), np.float32)
    for p in range(128):
        d = p % 64
        if d < 16:
            cosT[p] = np.cos(ang[d % 8])
            sinT[p] = np.sin(ang[d % 8])
    c["c_cos"] = cosT
    c["c_sin"] = sinT
    cc = np.arange(256)[:, None].astype(np.float64)
    cp = np.arange(256)[None, :].astype(np.float64)
    angc = 2 * np.pi * cc * cp / 256.0
    Cc = np.cos(angc) / 16.0
    Sc = -np.sin(angc) / 16.0
    cs = np.zeros((256, 512))
    for hf in range(2):
        cs[:, hf * 256:hf * 256 + 128] = Cc[:, hf * 128:(hf + 1) * 128]
        cs[:, hf * 256 + 128:hf * 256 + 256] = Sc[:, hf * 128:(hf + 1) * 128]
    c["c_cs"] = np.ascontiguousarray(cs.reshape(2, 128, 512).transpose(1, 0, 2).reshape(128, 1024)).astype(bf)
    a1 = 2 * np.pi * np.outer(np.arange(128), np.arange(128)) / 128.0
    C1, S1 = np.cos(a1), np.sin(a1)
    c["c_m1"] = np.concatenate([C1, -S1, S1, C1], axis=1).astype(bf)
    atw = 2 * np.pi * np.outer(np.arange(N2), np.arange(128)) / float(S)
    tw = np.concatenate([np.cos(atw).reshape(-1), np.sin(atw).reshape(-1)])
    c["c_tw"] = np.ascontiguousarray(np.broadcast_to(tw[None, :], (128, tw.size))).astype(bf)
    a3 = 2 * np.pi * np.outer(np.arange(N2), np.arange(N2)) / float(N2)
    sc = 1.0 / math.sqrt(S)
    K3c = np.kron(np.eye(G), np.cos(a3)) * sc
    K3s = np.kron(np.eye(G), np.sin(a3)) * sc
    c["c_k3"] = np.concatenate([K3c, K3s], axis=1).astype(bf)
    c["c_onesb"] = np.ones((128, 128), bf)
    c["c_ident"] = np.eye(128).astype(bf)
    return c


def _tile_w(W, nk, no):
    return np.ascontiguousarray(W.reshape(nk, 128, no, 128).transpose(2, 1, 0, 3)).reshape(no * 128, nk * 128)


def _prep_weights(inp):
    L = inp["w_in"].shape[0]
    out = {}
    for ab, sfx in (("a", "ffa"), ("b", "ffb")):
        wgu = np.empty((L, FC * 128, 2 * KC * 128), np.float32)
        wd = np.empty((L, KC * 128, FC * 128), np.float32)
        for l in range(L):
            g = _tile_w(np.asarray(inp[sfx + "_gate"][l]), KC, FC).reshape(FC, 128, 1, KC * 128)
            u = _tile_w(np.asarray(inp[sfx + "_up"][l]), KC, FC).reshape(FC, 128, 1, KC * 128)
            wgu[l] = np.concatenate([g, u], axis=2).reshape(FC * 128, 2 * KC * 128)
            wd[l] = _tile_w(np.asarray(inp[sfx + "_down"][l]), FC, KC)
        out["wgu_" + ab] = wgu
        out["wd_" + ab] = wd
    out["win"] = np.stack([_tile_w(np.asarray(inp["w_in"][l]), KC, 64) for l in range(L)])
    out["pf"] = np.stack([_tile_w(np.asarray(inp["p_f"][l]), 8, 16) for l in range(L)])
    out["pa"] = np.stack([_tile_w(np.asarray(inp["p_a"][l]), 8, 16) for l in range(L)])
    out["wo"] = np.stack([_tile_w(np.asarray(inp["w_o"][l]), 16, 16) for l in range(L)])
    gains = np.empty((L, 128, 64), np.float32)
    small = np.zeros((L, 128, 4), np.float32)
    lamv = np.empty((L, 128, 256), np.float32)
    for l in range(L):
        for gi, nm in enumerate(("norm_ffa", "norm_mix", "norm_ffb", "norm_out")):
            gains[l, :, gi * 16:(gi + 1) * 16] = np.asarray(inp[nm][l]).reshape(16, 128).T
        small[l, :, 0] = np.tile(np.asarray(inp["q_norm"][l]), 2)
        small[l, :, 1] = np.tile(np.asarray(inp["k_norm"][l]), 2)
        small[l, :, 2] = np.asarray(inp["subln"][l])
        lv = np.concatenate([np.asarray(inp[k][l]) for k in ("lambda_q1", "lambda_k1", "lambda_q2", "lambda_k2")])
        lamv[l] = np.broadcast_to(lv[None, :], (128, 256))
    out["gains"] = gains
    out["small"] = small
    out["lamv"] = lamv
    return out


def kernel(**inputs):
    x = np.asarray(inputs["x"], dtype=np.float32)
    B, S, _ = x.shape
    L = inputs["w_in"].shape[0]
    nc = build_nc(S, L)
    common = _prep_weights(inputs)
    common.update(_consts(S))
    in_maps = [dict(common, xT=np.ascontiguousarray(x[b].T)) for b in range(B)]
    res = run_bass_kernel_spmd(nc, in_maps, core_ids=list(range(B)))
    if DBG:
        kernel.dbg = res.results
    out = np.stack([np.ascontiguousarray(np.asarray(res.results[b]["outT"]).T) for b in range(B)])
    return out.astype(np.float32)
```
